# Optimizing a Trainium2 kernel written in Bass

```python
import math
import jax
import jax.numpy as jnp
from jax import lax
import numpy as np

D_MODEL = 1024
BATCH = 2
SEQ = 8192
DEPTH = 2

D_MIX = D_MODEL
GROUP_W = D_MIX // 4
HEAD_DIM = 64
N_GROUP_HEADS = GROUP_W // HEAD_DIM
MLA_Q_RANK = 256
MLA_KV_RANK = 128
MLA_NOPE = HEAD_DIM
MLA_ROPE = 32
MLA_V = HEAD_DIM
ROPE_THETA = 10000.0
Q_BLOCK = 128
HGRN_CHUNK = 64
S5_CH = 16
S5_GROUPS = GROUP_W // S5_CH
S5_P = 64
DT_MIN = 0.001
DT_MAX = 0.1
D_FF = 2816
CONV_W = 3
PLE_DIM = 256
EPS = 1e-6
N_IN = (MLA_Q_RANK + MLA_KV_RANK + MLA_ROPE) + (3 * GROUP_W + N_GROUP_HEADS) + 4 * GROUP_W + GROUP_W

kernel_name = 'hybrid_mla_fox_hgrn2_s5_block'


def rmsnorm(x, g):
    xf = x.astype(jnp.float32)
    y = xf * lax.rsqrt(jnp.mean(xf * xf, axis=-1, keepdims=True) + EPS)
    return (y * g.astype(jnp.float32)).astype(x.dtype)


def rope_tables(positions):
    half = MLA_ROPE // 2
    inv_freq = ROPE_THETA ** (-jnp.arange(half, dtype=jnp.float32) / half)
    ang = positions.astype(jnp.float32)[..., None] * inv_freq
    return jnp.cos(ang)[:, :, None, :], jnp.sin(ang)[:, :, None, :]


def apply_rope(t, cos, sin):
    half = MLA_ROPE // 2
    tf = t.astype(jnp.float32)
    t1, t2 = tf[..., :half], tf[..., half:]
    return jnp.concatenate([t1 * cos - t2 * sin, t1 * sin + t2 * cos], axis=-1).astype(t.dtype)


def causal_block_attention(q, k, v, scale, cum=None):
    b, s, h, _ = q.shape
    nb = s // Q_BLOCK
    q_blocks = q.reshape(b, nb, Q_BLOCK, h, q.shape[-1]).swapaxes(0, 1)
    key_pos = jnp.arange(s)

    def one_block(args):
        i, q_i = args[0], args[1]
        logits = jnp.einsum('bqhd,bkhd->bhqk', q_i, k).astype(jnp.float32) * scale
        if cum is not None:
            logits = logits + args[2][..., None] - cum[:, :, None, :]
        q_pos = i * Q_BLOCK + jnp.arange(Q_BLOCK)
        logits = jnp.where(key_pos[None, :] <= q_pos[:, None], logits, -jnp.inf)
        probs = jax.nn.softmax(logits, axis=-1).astype(v.dtype)
        return jnp.einsum('bhqk,bkhd->bqhd', probs, v)

    xs = (jnp.arange(nb), q_blocks)
    if cum is not None:
        xs = xs + (cum.reshape(b, h, nb, Q_BLOCK).transpose(2, 0, 1, 3),)
    out = lax.map(one_block, xs)
    return out.swapaxes(0, 1).reshape(b, s, h, v.shape[-1])


def mla_group(c_q, c_kv, k_rope, positions, q_norm_g, w_uq, kv_norm_g, w_ukv):
    b, s, _ = c_q.shape
    q = (rmsnorm(c_q, q_norm_g) @ w_uq).reshape(b, s, N_GROUP_HEADS, MLA_NOPE + MLA_ROPE)
    kv = (rmsnorm(c_kv, kv_norm_g) @ w_ukv).reshape(b, s, N_GROUP_HEADS, MLA_NOPE + MLA_V)
    cos, sin = rope_tables(positions)
    q_pe = apply_rope(q[..., MLA_NOPE:], cos, sin)
    k_pe = apply_rope(k_rope[:, :, None, :], cos, sin)
    q_full = jnp.concatenate([q[..., :MLA_NOPE], q_pe], axis=-1)
    k_full = jnp.concatenate([kv[..., :MLA_NOPE], jnp.broadcast_to(k_pe, (b, s, N_GROUP_HEADS, MLA_ROPE))], axis=-1)
    o = causal_block_attention(q_full, k_full, kv[..., MLA_NOPE:], (MLA_NOPE + MLA_ROPE) ** -0.5)
    return o.reshape(b, s, GROUP_W)


def fox_group(qkv, f_logit, b_f):
    b, s, _ = qkv.shape
    qkv = qkv.reshape(b, s, 3, N_GROUP_HEADS, HEAD_DIM)
    log_f = jax.nn.log_sigmoid(f_logit.astype(jnp.float32) + b_f.astype(jnp.float32))
    cum = jnp.cumsum(log_f, axis=1).transpose(0, 2, 1)
    o = causal_block_attention(qkv[:, :, 0], qkv[:, :, 1], qkv[:, :, 2], HEAD_DIM ** -0.5, cum)
    return o.reshape(b, s, GROUP_W)


def hgrn2_group(q, f_logit, i_in, lb):
    b, s, _ = q.shape
    nc = s // HGRN_CHUNK
    z = f_logit.astype(jnp.float32)
    lbf = lb.astype(jnp.float32)
    log_f = jnp.logaddexp(jnp.log(lbf), jnp.log1p(-lbf) + jax.nn.log_sigmoid(z))
    k = (1.0 - lbf) * jax.nn.sigmoid(-z)

    def to_chunks(t):
        return t.astype(jnp.float32).reshape(b, nc, HGRN_CHUNK, N_GROUP_HEADS, HEAD_DIM).transpose(1, 0, 3, 2, 4)

    causal = jnp.tril(jnp.ones((HGRN_CHUNK, HGRN_CHUNK), dtype=bool))

    def chunk_step(state, inp):
        q_c, k_c, v_c, lf_c = inp
        bcum = jnp.cumsum(lf_c, axis=2)
        diff = bcum[:, :, :, None, :] - bcum[:, :, None, :, :]
        decay = jnp.exp(jnp.where(causal[:, :, None], diff, -jnp.inf))
        scores = jnp.einsum('bhtk,bhsk,bhtsk->bhts', q_c, k_c, decay)
        o = jnp.einsum('bhts,bhsv->bhtv', scores, v_c) + jnp.einsum('bhtk,bhkv->bhtv', q_c * jnp.exp(bcum), state)
        b_last = bcum[:, :, -1:, :]
        state = jnp.exp(b_last[:, :, 0, :, None]) * state + jnp.einsum('bhsk,bhsv->bhkv', k_c * jnp.exp(b_last - bcum), v_c)
        return state, o

    state0 = jnp.zeros((b, N_GROUP_HEADS, HEAD_DIM, HEAD_DIM), jnp.float32)
    _, o = lax.scan(chunk_step, state0, (to_chunks(q), to_chunks(k), to_chunks(i_in), to_chunks(log_f)))
    return o.transpose(1, 0, 3, 2, 4).reshape(b, s, GROUP_W).astype(q.dtype)


def s5_combine(earlier, later):
    a_re, a_im, x_re, x_im = earlier
    b_re, b_im, y_re, y_im = later
    return (b_re * a_re - b_im * a_im, b_re * a_im + b_im * a_re,
            b_re * x_re - b_im * x_im + y_re, b_re * x_im + b_im * x_re + y_im)


def s5_group(u, lam_re, lam_im, log_step, b_re, b_im, c_re, c_im, d_skip, w_glu, b_glu):
    b, s, _ = u.shape
    uf = u.astype(jnp.float32)
    step = jnp.exp(log_step.astype(jnp.float32))[:, None]
    lre = jnp.minimum(lam_re.astype(jnp.float32), -1e-4)
    lim = lam_im.astype(jnp.float32)
    mag = jnp.exp(lre * step)
    a_re, a_im = mag * jnp.cos(lim * step), mag * jnp.sin(lim * step)
    den = lre * lre + lim * lim
    coef_re = ((a_re - 1.0) * lre + a_im * lim) / den
    coef_im = (a_im * lre - (a_re - 1.0) * lim) / den
    br, bi = b_re.astype(jnp.float32), b_im.astype(jnp.float32)
    bb_re = coef_re[..., None] * br - coef_im[..., None] * bi
    bb_im = coef_re[..., None] * bi + coef_im[..., None] * br
    ug = uf.reshape(b, s, S5_GROUPS, S5_CH)
    bu_re = jnp.einsum('bsgh,gph->bsgp', ug, bb_re)
    bu_im = jnp.einsum('bsgh,gph->bsgp', ug, bb_im)
    _, _, x_re, x_im = lax.associative_scan(
        s5_combine,
        (jnp.broadcast_to(a_re, bu_re.shape), jnp.broadcast_to(a_im, bu_re.shape), bu_re, bu_im),
        axis=1)
    y = jnp.einsum('bsgp,ghp->bsgh', x_re, c_re.astype(jnp.float32)) - jnp.einsum('bsgp,ghp->bsgh', x_im, c_im.astype(jnp.float32))
    y = y.reshape(b, s, GROUP_W) + d_skip.astype(jnp.float32) * uf
    zact = jax.nn.gelu(y)
    out = zact * jax.nn.sigmoid(zact @ w_glu.astype(jnp.float32) + b_glu.astype(jnp.float32))
    return out.astype(u.dtype)


def conv_ffn(x, w_up, conv_w, conv_b, w_down):
    s = x.shape[1]
    up = x @ w_up
    padded = jnp.pad(up, ((0, 0), (CONV_W - 1, 0), (0, 0)))
    conv = conv_b + conv_w[0] * padded[:, 0:s]
    for j in range(1, CONV_W):
        conv = conv + conv_w[j] * padded[:, j:j + s]
    gate, val = jnp.split(conv, 2, axis=-1)
    return (jax.nn.silu(gate) * val) @ w_down


def setup_inputs(seed: int = 0) -> dict:
    key = jax.random.key(seed)
    ks = iter(jax.random.split(key, 40))
    nrm = lambda shape, scale: scale * jax.random.normal(next(ks), shape, jnp.float32)
    gain = lambda shape: 1.0 + nrm(shape, 0.02)
    L = DEPTH
    inp = {}
    inp['x'] = nrm((BATCH, SEQ, D_MODEL), 1.0)
    inp['p'] = nrm((DEPTH, BATCH, SEQ, PLE_DIM), 1.0)
    offs = jax.random.randint(next(ks), (BATCH, 1), 0, 4096, dtype=jnp.int32)
    inp['positions'] = (jnp.arange(SEQ, dtype=jnp.int32)[None, :] + offs).astype(jnp.int32)
    inp['attn_norm_g'] = gain((L, D_MODEL))
    inp['w_in'] = nrm((L, D_MODEL, N_IN), D_MODEL ** -0.5)
    inp['mla_q_norm_g'] = gain((L, MLA_Q_RANK))
    inp['mla_w_uq'] = nrm((L, MLA_Q_RANK, N_GROUP_HEADS * (MLA_NOPE + MLA_ROPE)), MLA_Q_RANK ** -0.5)
    inp['mla_kv_norm_g'] = gain((L, MLA_KV_RANK))
    inp['mla_w_ukv'] = nrm((L, MLA_KV_RANK, N_GROUP_HEADS * (MLA_NOPE + MLA_V)), MLA_KV_RANK ** -0.5)
    inp['fox_b_f'] = jnp.linspace(1.0, 6.0, N_GROUP_HEADS, dtype=jnp.float32)[None, :] + nrm((L, N_GROUP_HEADS), 0.1)
    inp['hgrn_lb_param'] = nrm((L, GROUP_W), 0.1)
    inp['s5_lam_re'] = -0.5 + nrm((L, S5_GROUPS, S5_P), 0.01)
    inp['s5_lam_im'] = jnp.broadcast_to(math.pi * jnp.arange(S5_P, dtype=jnp.float32), (L, S5_GROUPS, S5_P))
    inp['s5_log_step'] = jax.random.uniform(next(ks), (L, S5_GROUPS), jnp.float32, math.log(DT_MIN), math.log(DT_MAX))
    inp['s5_b_re'] = nrm((L, S5_GROUPS, S5_P, S5_CH), (2 * S5_CH) ** -0.5)
    inp['s5_b_im'] = nrm((L, S5_GROUPS, S5_P, S5_CH), (2 * S5_CH) ** -0.5)
    inp['s5_c_re'] = nrm((L, S5_GROUPS, S5_CH, S5_P), (2 * S5_P) ** -0.5)
    inp['s5_c_im'] = nrm((L, S5_GROUPS, S5_CH, S5_P), (2 * S5_P) ** -0.5)
    inp['s5_d'] = nrm((L, GROUP_W), 1.0)
    inp['s5_w_glu'] = nrm((L, GROUP_W, GROUP_W), GROUP_W ** -0.5)
    inp['s5_b_glu'] = nrm((L, GROUP_W), 0.01)
    inp['group_norm_g'] = gain((L, D_MIX))
    inp['w_out'] = nrm((L, D_MIX, D_MODEL), D_MIX ** -0.5)
    inp['ffn_norm_g'] = gain((L, D_MODEL))
    inp['w_up'] = nrm((L, D_MODEL, 2 * D_FF), D_MODEL ** -0.5)
    inp['conv_w'] = nrm((L, CONV_W, 2 * D_FF), CONV_W ** -0.5)
    inp['conv_b'] = nrm((L, 2 * D_FF), 0.01)
    inp['w_down'] = nrm((L, D_FF, D_MODEL), D_FF ** -0.5)
    inp['ple_norm_g'] = gain((L, D_MODEL))
    inp['w_ple_gate'] = nrm((L, D_MODEL, D_MODEL), D_MODEL ** -0.5)
    inp['w_ple'] = nrm((L, PLE_DIM, D_MODEL), PLE_DIM ** -0.5)
    inp['final_norm_g'] = gain((D_MODEL,))
    return inp


def reference(x, p, positions, attn_norm_g, w_in, mla_q_norm_g, mla_w_uq, mla_kv_norm_g, mla_w_ukv,
              fox_b_f, hgrn_lb_param, s5_lam_re, s5_lam_im, s5_log_step, s5_b_re, s5_b_im, s5_c_re, s5_c_im,
              s5_d, s5_w_glu, s5_b_glu, group_norm_g, w_out, ffn_norm_g, w_up, conv_w, conv_b, w_down,
              ple_norm_g, w_ple_gate, w_ple, final_norm_g):
    lb_all = jnp.cumsum(jax.nn.softmax(hgrn_lb_param.astype(jnp.float32), axis=0), axis=0)
    lb_all = lb_all - lb_all[0:1]
    sizes = (MLA_Q_RANK, MLA_KV_RANK, MLA_ROPE, 3 * GROUP_W, N_GROUP_HEADS, GROUP_W, GROUP_W, GROUP_W, GROUP_W, GROUP_W)
    offsets = [int(o) for o in np.cumsum(sizes)[:-1]]
    gw = GROUP_W
    h = x
    for i in range(DEPTH):
        hn = rmsnorm(h, attn_norm_g[i])
        proj = hn @ w_in[i]
        (c_q, c_kv, k_rope, fox_qkv, fox_f, hg_q, hg_f, hg_i, hg_g, s5_u) = jnp.split(proj, offsets, axis=-1)
        y_a = mla_group(c_q, c_kv, k_rope, positions, mla_q_norm_g[i], mla_w_uq[i], mla_kv_norm_g[i], mla_w_ukv[i])
        y_b = fox_group(fox_qkv, fox_f, fox_b_f[i])
        y_c = hgrn2_group(hg_q, hg_f, hg_i, lb_all[i])
        y_d = s5_group(s5_u, s5_lam_re[i], s5_lam_im[i], s5_log_step[i], s5_b_re[i], s5_b_im[i],
                       s5_c_re[i], s5_c_im[i], s5_d[i], s5_w_glu[i], s5_b_glu[i])
        gn = group_norm_g[i]
        y = jnp.concatenate([
            rmsnorm(y_a, gn[0:gw]),
            rmsnorm(y_b, gn[gw:2 * gw]),
            rmsnorm(y_c, gn[2 * gw:3 * gw]) * jax.nn.sigmoid(hg_g),
            rmsnorm(y_d, gn[3 * gw:4 * gw]),
        ], axis=-1)
        h = h + y @ w_out[i]
        h = h + conv_ffn(rmsnorm(h, ffn_norm_g[i]), w_up[i], conv_w[i], conv_b[i], w_down[i])
        ple_gate = jax.nn.sigmoid(rmsnorm(h, ple_norm_g[i]) @ w_ple_gate[i])
        h = h + ple_gate * (p[i] @ w_ple[i])
    return rmsnorm(h, final_norm_g)
```

```python
import math
import os
from contextlib import ExitStack
import numpy as np
import concourse.bass as bass
import concourse.mybir as mybir
from concourse.bass_utils import run_bass_kernel_spmd

F32 = mybir.dt.float32
BF16 = mybir.dt.bfloat16
I32 = mybir.dt.int32
AF = mybir.ActivationFunctionType
ALU = mybir.AluOpType

NT = 8192
D = 1024
DEPTH = 2
TT = 512
NTT = NT // TT
EPS = 1e-6
DFF = 2816
NIN_FM = 18
TWO_PI = 6.283185307179586
DBG = []


class Sched:
    EPOCH = 16000
    DEPOCH = 30000
    K = 4

    def __init__(self, nc, st):
        self.nc, self.st = nc, st
        self.nsem = 0
        self.eng = {}
        for name, h in (("pe", nc.tensor), ("act", nc.scalar), ("dve", nc.vector), ("pool", nc.gpsimd), ("sp", nc.sync)):
            self.eng[name] = dict(h=h, sem=self._sem(), cnt=0, seen={}, last=None)
        self.dq = {q: dict(slots=[dict(sem=self._sem(), val=0) for _ in range(self.K)], n=0) for q in ("sp", "act", "pool")}
        self.last_w, self.reads = {}, {}

    def _sem(self):
        self.nsem += 1
        s = self.st.enter_context(self.nc.semaphore(f"sm{self.nsem}"))
        return (self.nsem, s)

    def _wait(self, en, ev):
        (sid, sem), val, src = ev
        e = self.eng[en]
        if e["seen"].get(sid, 0) >= val:
            return
        e["h"].wait_ge(sem, val)
        e["seen"][sid] = val

    def _deps(self, en, reads, writes):
        for k in reads:
            w = self.last_w.get(k)
            if w is not None:
                self._wait(en, w)
        for k in writes:
            w = self.last_w.get(k)
            if w is not None and not (w[2] == en == "pe"):
                self._wait(en, w)
            for ev in self.reads.get(k, {}).values():
                if not (ev[2] == en == "pe"):
                    self._wait(en, ev)

    def _record(self, ev, reads, writes):
        for k in writes:
            self.last_w[k] = ev
            self.reads[k] = {}
        for k in reads:
            self.reads.setdefault(k, {})[ev[0][0]] = ev

    def op(self, en, reads, writes, fn):
        e = self.eng[en]
        self._deps(en, reads, writes)
        ins = fn(e["h"])
        if e["cnt"] >= self.EPOCH:
            e["sem"], e["cnt"] = self._sem(), 0
        e["cnt"] += 1
        ins.then_inc(e["sem"][1], 1)
        ev = (e["sem"], e["cnt"], en)
        e["last"] = ev
        self._record(ev, reads, writes)

    def dma(self, q, out, in_, reads, writes):
        d = self.dq[q]
        self._deps(q, reads, writes)
        sl = d["slots"][d["n"] % self.K]
        d["n"] += 1
        if sl["val"] > 0:
            self._wait(q, (sl["sem"], sl["val"], "dma"))
        if sl["val"] >= self.DEPOCH:
            sl["sem"], sl["val"] = self._sem(), 0
        self.eng[q]["h"].dma_start(out=out, in_=in_).then_inc(sl["sem"][1], 16)
        sl["val"] += 16
        ev = (sl["sem"], sl["val"], "dma")
        self._record(ev, reads, writes)

    def barrier(self):
        evs = [e["last"] for e in self.eng.values() if e["last"] is not None]
        for d in self.dq.values():
            for sl in d["slots"]:
                if sl["val"] > 0:
                    evs.append((sl["sem"], sl["val"], "dma"))
        for en in self.eng:
            for ev in evs:
                self._wait(en, ev)
        self.last_w, self.reads = {}, {}


class Ctx:
    pass


def fm(ap, r0, nrows):
    if nrows % 128 == 0 and nrows > 128:
        return ap[r0:r0 + nrows, :].rearrange("(k p) n -> p k n", p=128)
    return ap[r0:r0 + nrows, :]


def build():
    nc = bass.Bass("TRN2", target_bir_lowering=False)
    C = Ctx()
    C.nc = nc
    di = lambda name, shape, dt=F32: nc.dram_tensor(name, list(shape), dt, kind="ExternalInput").ap()
    dbgset = set(filter(None, os.environ.get("KDBG", "").split(",")))
    C.limit = int(os.environ.get("KSTOP", "100000"))
    C.phase = 0
    ds_ = lambda name, shape, dt=BF16: nc.dram_tensor(name, list(shape), dt, kind=("ExternalOutput" if name in dbgset else "Internal")).ap()
    I = {}
    I["xT"] = di("xT", [D, NT])
    I["pT"] = di("pT", [DEPTH, 256, NT])
    I["pos"] = di("pos", [1, NT], I32)
    I["ident"] = di("ident", [128, 128])
    I["amask"] = di("amask", [128, 4, 512])
    I["cvec"] = di("cvec", [128, 16])
    I["amadd"] = di("amadd", [128, 4, 512])
    I["invf"] = di("invf", [32, 1])
    I["w_in_fm"] = di("w_in_fm", [DEPTH, D, NIN_FM * 128])
    I["w_in_tm"] = di("w_in_tm", [DEPTH, D, 512])
    I["w_uq"] = di("w_uq", [DEPTH, 256, 1024])
    I["w_uk"] = di("w_uk", [DEPTH, 128, 512])
    I["w_uv"] = di("w_uv", [DEPTH, 128, 256])
    I["w_out"] = di("w_out", [DEPTH, D, D])
    I["w_up"] = di("w_up", [DEPTH, D, 2 * DFF])
    I["w_down"] = di("w_down", [DEPTH, DFF, D])
    I["w_pg"] = di("w_pg", [DEPTH, D, D])
    I["w_ple"] = di("w_ple", [DEPTH, 256, D])
    I["w_glu"] = di("w_glu", [DEPTH, 256, 256])
    I["vecs"] = di("vecs", [DEPTH, 128, 64])
    I["convp"] = di("convp", [DEPTH, 128, 44, 4])
    I["s5a"] = di("s5a", [DEPTH, 128, 3, 16])
    I["s5b"] = di("s5b", [DEPTH, 128, 2, 5, 64])
    I["s5c"] = di("s5c", [DEPTH, 128, 16, 16])
    I["lbp"] = di("lbp", [128, 2, 2])
    I["fing"] = di("fing", [128, 8])
    out = nc.dram_tensor("out", [D, NT], F32, kind="ExternalOutput").ap()
    S = {}
    S["H"] = ds_("H", [D, NT], F32)
    S["HN"] = ds_("HN", [D, NT])
    S["PF"] = ds_("PF", [NIN_FM * 128, NT])
    S["PF32"] = ds_("PF32", [384, NT], F32)
    S["PTM"] = ds_("PTM", [NT, 512])
    S["CQN"] = ds_("CQN", [256, NT])
    S["KVN"] = ds_("KVN", [128, NT])
    S["QA"] = ds_("QA", [4, 128, NT])
    S["KA"] = ds_("KA", [4, 128, NT])
    S["VM"] = ds_("VM", [NT, 256])
    S["ROPE"] = ds_("ROPE", [2, 32, NT], F32)
    S["FAUG"] = ds_("FAUG", [4, 3, NT])
    S["FBIAS"] = ds_("FBIAS", [128, 64, 4], F32)
    S["Y"] = ds_("Y", [D, NT])
    S["YS5"] = ds_("YS5", [256, NT], F32)
    S["YN"] = ds_("YN", [D, NT])
    S["ACT"] = ds_("ACTB", [DFF, NT])
    S["PB"] = ds_("PB", [256, NT])

    with ExitStack() as st:
        sc = Sched(nc, st)
        C.sc = sc
        sb = lambda name, shape, dt=F32: st.enter_context(nc.sbuf_tensor("c_" + name, list(shape), dt))
        C.psall = st.enter_context(nc.psum_tensor("psall", [128, 4096], F32))
        C.ps = [C.psall[:, i * 512:(i + 1) * 512] for i in range(8)]
        C.ident = sb("ident", [128, 128])
        C.identb = sb("identb", [128, 128], BF16)
        C.onesb = sb("onesb", [128, 128], BF16)
        C.onesf = sb("onesf", [128, 2048])
        C.amask = sb("amaskb", [128, 4, 512], BF16)
        C.cvec = sb("cvec", [128, 16])
        C.amadd = sb("amadd", [128, 4, 512])
        C.vecs = sb("vecs", [128, 64])
        sc.dma("sp", C.ident[:], I["ident"], [], ["ident"])
        sc.dma("pool", C.identb[:], I["ident"], [], ["identb"])
        sc.dma("pool", C.amask[:], I["amask"], [], ["amask"])
        sc.dma("sp", C.cvec[:], I["cvec"], [], ["cvec"])
        sc.dma("sp", C.amadd[:], I["amadd"], [], ["amadd"])
        sc.op("dve", [], ["onesb"], lambda h: h.memset(C.onesb[:], 1.0))
        sc.op("dve", [], ["onesf"], lambda h: h.memset(C.onesf[:], 1.0))
        for k in range(16):
            sc.dma("sp", S["H"][k * 64:(k + 1) * 64, :], I["xT"][k * 64:(k + 1) * 64, :], [], ["H"])
        rope_tables(C, I, S)
        sc.barrier()
        for L in range(DEPTH):
            sc.dma("sp", C.vecs[:], I["vecs"][L], [], ["vecs"])
            layer(C, I, S, L)
        with ExitStack() as ph:
            C.ph = ph
            g = psb(C, "fg", [128, 8])
            sc.dma("sp", g[:], I["fing"], [], ["fg"])
            rmsnorm_fm(C, S["H"], 0, 8, g, ["fg"], out, 0, F32, 1024)
        sc.barrier()
    return nc


_UID = [0]


def psb(C, name, shape, dt=F32):
    _UID[0] += 1
    return C.ph.enter_context(C.nc.sbuf_tensor(f"{name}_u{_UID[0]}", list(shape), dt))


def rmsnorm_fm(C, src, r0, KT, g, gkeys, dst, d0, ddt, Dn, gcol0=0, post=None, tag="rn"):
    sc, nc = C.sc, C.nc
    sdt = src.tensor.dtype if hasattr(src, "tensor") else F32
    xs = [psb(C, f"{tag}x{i}", [128, KT, TT], sdt) for i in range(2)]
    sq = [psb(C, f"{tag}q{i}", [128, KT, TT], BF16) for i in range(2)]
    rs = [psb(C, f"{tag}r{i}", [128, TT]) for i in range(2)]
    os_ = [psb(C, f"{tag}o{i}", [128, KT, TT], ddt) for i in range(2)]
    for t in range(NTT):
        i = t % 2
        x, q, r, o = xs[i], sq[i], rs[i], os_[i]
        kx, kq, kr, ko = f"{tag}x{i}", f"{tag}q{i}", f"{tag}r{i}", f"{tag}o{i}"
        pk = f"ps{6 + i}"
        pst = C.ps[6 + i]
        cs = slice(t * TT, (t + 1) * TT)
        sv = src[r0:r0 + KT * 128, cs]
        sv = sv.rearrange("(k p) n -> p k n", p=128)
        sc.dma("pool", x[:], sv, ["H", "Y", "PF"], [kx])
        sc.op("act", [kx], [kq], lambda h: h.activation(out=q[:], in_=x[:], func=AF.Square))
        def mm(h):
            for k in range(KT):
                ins = h.matmul(pst[:], C.onesb[:], q[:, k, :], start=(k == 0), stop=(k == KT - 1))
            return ins
        sc.op("pe", [kq, "onesb"], [pk], mm)
        sc.op("act", [pk], [kr], lambda h: h.activation(out=r[:], in_=pst[:], func=AF.Sqrt, scale=1.0 / Dn, bias=C.cvec[:, 0:1]))
        sc.op("dve", [kr], [kr], lambda h: h.reciprocal(out=r[:], in_=r[:]))
        for k in range(KT):
            sc.op("dve", [kx, kr] + gkeys, [ko], lambda h, k=k: h.scalar_tensor_tensor(
                out=o[:, k, :], in0=x[:, k, :], scalar=g[:, gcol0 + k:gcol0 + k + 1], in1=r[:], op0=ALU.mult, op1=ALU.mult))
        if post is not None:
            post(o, ko, t)
        dv = dst[d0:d0 + KT * 128, cs].rearrange("(k p) n -> p k n", p=128)
        sc.dma("sp", dv, o[:], [ko], ["HN", "YN", "CQN", "KVN", "out"])


def linear_fm(C, src, r0, KT, W, n_tiles, TG, evac, tag="lf", after=None, WB=None, cast_eng="pool"):
    sc, nc = C.sc, C.nc
    WB = WB or (4 if KT <= 8 else 1)
    a = psb(C, f"{tag}a", [128, KT, TG], BF16)
    ws = [psb(C, f"{tag}w{i}", [128, KT, 128], BF16) for i in range(3)]
    stg = [psb(C, f"{tag}s{i}", [128, KT, WB * 128]) for i in range(2)]
    Wv = W.rearrange("(k p) n -> p k n", p=128)
    blocks = [(g, jb) for g in range(NT // TG) for jb in range(0, n_tiles, WB)]
    def load_block(bi):
        g, jb = blocks[bi]
        nb = min(WB, n_tiles - jb)
        sc.dma("act", stg[bi % 2][:, :, 0:nb * 128], Wv[:, :, jb * 128:(jb + nb) * 128], [], [f"{tag}s{bi % 2}"])
    load_block(0)
    cnt = 0
    for bi, (g, jb) in enumerate(blocks):
        if jb == 0:
            sv = src[r0:r0 + KT * 128, g * TG:(g + 1) * TG].rearrange("(k p) n -> p k n", p=128)
            sc.dma("pool", a[:], sv, ["HN", "YN", "ACT", "CQN", "KVN"], [f"{tag}a"])
        if bi + 1 < len(blocks):
            load_block(bi + 1)
        nb = min(WB, n_tiles - jb)
        for jj in range(nb):
            j = jb + jj
            w = ws[j % 3]
            wk = f"{tag}w{j % 3}"
            if cast_eng == "act":
                sc.op("act", [f"{tag}s{bi % 2}"], [wk], lambda h, w=w, jj=jj, bi=bi: h.activation(out=w[:], in_=stg[bi % 2][:, :, jj * 128:(jj + 1) * 128], func=AF.Identity))
            else:
                sc.op("pool", [f"{tag}s{bi % 2}"], [wk], lambda h, w=w, jj=jj, bi=bi: h.tensor_copy(out=w[:], in_=stg[bi % 2][:, :, jj * 128:(jj + 1) * 128]))
            for t in range(TG // TT):
                pi = cnt % 4
                cnt += 1
                pst, pk = C.ps[pi], f"ps{pi}"
                def mm(h, t=t, w=w, pst=pst):
                    for k in range(KT):
                        ins = h.matmul(pst[:], w[:, k, :], a[:, k, t * TT:(t + 1) * TT], start=(k == 0), stop=(k == KT - 1))
                    return ins
                sc.op("pe", [wk, f"{tag}a"], [pk], mm)
                evac(pst, pk, j, g * TG + t * TT, cnt)
            if after is not None:
                after(j, g)


def store_evac(C, dst_of, tag="se"):
    sc = C.sc
    stg = {}
    def evac(pst, pk, j, tok0, cnt):
        ap, row0, dt = dst_of(j)
        i = cnt % 3
        key = f"{tag}{dt}{i}"
        if key not in stg:
            stg[key] = psb(C, key, [128, TT], dt)
        s = stg[key]
        if cnt % 2 == 0:
            sc.op("act", [pk], [key], lambda h: h.activation(out=s[:], in_=pst[:], func=AF.Identity))
        else:
            sc.op("dve", [pk], [key], lambda h: h.tensor_copy(out=s[:], in_=pst[:]))
        sc.dma("sp", ap[row0:row0 + 128, tok0:tok0 + TT], s[:], [key], ["PF", "PB"])
    return evac


def resid_evac(C, S, tag="re", mul=None):
    sc = C.sc
    hb = [psb(C, f"{tag}h{i}", [128, TT]) for i in range(3)]
    def evac(pst, pk, j, tok0, cnt):
        i = cnt % 3
        h_, hk = hb[i], f"{tag}h{i}"
        hv = S["H"][j * 128:(j + 1) * 128, tok0:tok0 + TT]
        sc.dma("pool", h_[:], hv, ["H"], [hk])
        sc.op("dve", [pk, hk], [hk], lambda h: h.tensor_tensor(out=h_[:], in0=pst[:], in1=h_[:], op=ALU.add))
        sc.dma("sp", hv, h_[:], [hk], ["H"])
    return evac


def rope_tables(C, I, S):
    sc, nc = C.sc, C.nc
    with ExitStack() as ph:
        C.ph = ph
        pi = psb(C, "rp_i", [32, 2048], I32)
        pf = psb(C, "rp_f", [32, 2048])
        an = psb(C, "rp_a", [32, 2048])
        kk = psb(C, "rp_k", [32, 2048], I32)
        kf = psb(C, "rp_kf", [32, 2048])
        o = psb(C, "rp_o", [32, 2048])
        invf = psb(C, "rp_inv", [32, 1])
        sc.dma("sp", invf[:], I["invf"], [], ["invf"])
        for g in range(4):
            cs = slice(g * 2048, (g + 1) * 2048)
            src = bass.AP(I["pos"].tensor, g * 2048, [[0, 32], [1, 2048]])
            sc.dma("sp", pi[:], src, [], ["rp_i"])
            sc.op("dve", ["rp_i"], ["rp_f"], lambda h: h.tensor_copy(out=pf[:], in_=pi[:]))
            sc.op("dve", ["rp_f", "invf"], ["rp_f"], lambda h: h.tensor_scalar(out=pf[:], in0=pf[:], scalar1=invf[:, 0:1], scalar2=None, op0=ALU.mult))
            for which, shift in ((0, math.pi / 2), (1, 0.0)):
                sc.op("dve", ["rp_f"], ["rp_a"], lambda h: h.tensor_scalar(out=an[:], in0=pf[:], scalar1=shift, scalar2=None, op0=ALU.add))
                sc.op("dve", ["rp_a"], ["rp_k"], lambda h: h.tensor_scalar(out=kk[:], in0=an[:], scalar1=1.0 / TWO_PI, scalar2=None, op0=ALU.mult))
                sc.op("dve", ["rp_k"], ["rp_kf"], lambda h: h.tensor_copy(out=kf[:], in_=kk[:]))
                sc.op("dve", ["rp_kf", "rp_a"], ["rp_a"], lambda h: h.scalar_tensor_tensor(out=an[:], in0=kf[:], scalar=-TWO_PI, in1=an[:], op0=ALU.mult, op1=ALU.add))
                sc.op("dve", ["rp_a"], ["rp_a"], lambda h: h.tensor_scalar(out=an[:], in0=an[:], scalar1=3.1415925, scalar2=-3.1415925, op0=ALU.min, op1=ALU.max))
                sc.op("act", ["rp_a"], ["rp_o"], lambda h: h.activation(out=o[:], in_=an[:], func=AF.Sin))
                if which == 1:
                    sc.op("dve", ["rp_o", "cvec"], ["rp_o"], lambda h: h.tensor_scalar(out=o[:], in0=o[:], scalar1=C.cvec[0:32, 1:2], scalar2=None, op0=ALU.mult))
                sc.dma("sp", S["ROPE"][which, :, cs], o[:], ["rp_o"], ["ROPE"])


def attention_multi(C, S, heads, use_bias):
    sc, nc = C.sc, C.nc
    with ExitStack() as ph:
        C.ph = ph
        QTs = [psb(C, f"aQ{i}", [128, NT], BF16) for i in range(2)]
        KTs = [psb(C, f"aK{i}", [128, NT], BF16) for i in range(2)]
        VAs = [psb(C, f"aV{i}", [128, 64, 128], BF16) for i in range(2)]
        for i in range(2):
            sc.op("pool", [], [f"aQ{i}"], lambda h, i=i: h.memset(QTs[i][:], 0.0))
            sc.op("pool", [], [f"aK{i}"], lambda h, i=i: h.memset(KTs[i][:], 0.0))
            sc.op("pool", [], [f"aV{i}"], lambda h, i=i: h.memset(VAs[i][:], 0.0))
        P = [psb(C, f"aP{i}", [128, 1024], BF16) for i in range(3)]
        osb = [psb(C, f"aO{i}", [65, TT]) for i in range(2)]
        rec = [psb(C, f"aR{i}", [64, TT]) for i in range(2)]
        yb = [psb(C, f"aY{i}", [64, TT], BF16) for i in range(2)]
        dtmp = [psb(C, f"aD{i}", [128, TT]) for i in range(2)]
        bias = None
        if use_bias:
            bias = psb(C, "aB", [128, 64, 4])
            sc.dma("pool", bias[:], S["FBIAS"], ["FBIAS"], ["aB"])

        def loads(hi):
            H = heads[hi]
            b_ = hi % 2
            QT, KT_, VA = QTs[b_], KTs[b_], VAs[b_]
            for ap, p0, n in H["q"]:
                sc.dma("pool", QT[p0:p0 + n, :], ap, ["PF", "QA", "FAUG"], [f"aQ{b_}"])
            for ap, p0, n in H["k"]:
                sc.dma("pool", KT_[p0:p0 + n, :], ap, ["PF", "KA"], [f"aK{b_}"])
            if H["kones"] is not None:
                k0, kn = H["kones"]
                sc.op("dve", [], [f"aK{b_}"], lambda h: h.memset(KT_[k0:k0 + kn, :], 1.0))
            sc.op("dve", [], [f"aV{b_}"], lambda h: h.memset(VA[:, :, 64:65], 1.0))
            for b4 in range(4):
                vv = H["v"][b4 * 2048:(b4 + 1) * 2048, H["vcol"]:H["vcol"] + 64].rearrange("(b p) d -> p b d", p=128)
                sc.dma("pool", VA[:, b4 * 16:(b4 + 1) * 16, 0:64], vv, ["PTM", "VM"], [f"aV{b_}"])

        loads(0)
        gidx = 0
        ecnt = 0
        for hi, H in enumerate(heads):
            if hi + 1 < len(heads):
                loads(hi + 1)
            b_ = hi % 2
            QT, KT_, VA = QTs[b_], KTs[b_], VAs[b_]
            kQ, kK, kV = f"aQ{b_}", f"aK{b_}", f"aV{b_}"
            Cq, scale, bh = H["Cq"], H["scale"], H["bias_head"]
            items = [(qt, kp) for qt in range(NTT) for kp in range(2 * (qt + 1))]

            def emit_qk_exp(qt, kp, g):
                si = g % 2
                psA, psB = C.ps[2 * si], C.ps[2 * si + 1]
                pk2 = [f"ps{2 * si}", f"ps{2 * si + 1}"]
                pp, ppk = P[g % 3], f"aP{g % 3}"
                qs = slice(qt * TT, (qt + 1) * TT)
                def mm(h):
                    for m, pst in ((0, psA), (1, psB)):
                        kb = 2 * kp + m
                        ins = h.matmul(pst[:], KT_[:, kb * 128:(kb + 1) * 128], QT[:, qs], start=True, stop=True)
                    return ins
                sc.op("pe", [kQ, kK], pk2, mm)
                if bias is None and 2 * kp + 1 < 4 * qt:
                    both = C.psall[:, si * 1024:(si + 1) * 1024]
                    sc.op("act", pk2, [ppk], lambda h, both=both: h.activation(out=pp[:, 0:1024], in_=both, func=AF.Exp, scale=scale))
                    return
                for m, pst in ((0, psA), (1, psB)):
                    kb = 2 * kp + m
                    src_, srck = pst, pk2[m]
                    if kb >= 4 * qt:
                        mi = kb - 4 * qt
                        dt_, dk = dtmp[m], f"aD{m}"
                        sc.op("dve", [pk2[m], "amadd"], [dk], lambda h, pst=pst, mi=mi, dt_=dt_: h.tensor_tensor(
                            out=dt_[:], in0=pst[:], in1=C.amadd[:, mi, :], op=ALU.add))
                        src_, srck = dt_, dk
                    if bias is None:
                        sc.op("act", [srck], [ppk], lambda h, m=m, src_=src_: h.activation(out=pp[:, m * TT:(m + 1) * TT], in_=src_[:], func=AF.Exp, scale=scale))
                    else:
                        sc.op("act", [srck, "aB"], [ppk], lambda h, m=m, src_=src_, kb=kb: h.activation(
                            out=pp[:, m * TT:(m + 1) * TT], in_=src_[:], func=AF.Exp, scale=scale, bias=bias[:, kb, bh:bh + 1]))

            def emit_pv(qt, kp, g):
                nonlocal ecnt
                nkb = 4 * (qt + 1)
                po, pok = C.ps[4 + qt % 2], f"ps{4 + qt % 2}"
                pp, ppk = P[g % 3], f"aP{g % 3}"
                def pv(h):
                    for m in (0, 1):
                        kb = 2 * kp + m
                        ins = h.matmul(po[:, :], VA[:, kb, :], pp[:, m * TT:(m + 1) * TT], start=(kb == 0), stop=(kb == nkb - 1))
                    return ins
                sc.op("pe", [ppk, kV], [pok], pv)
                if kp == 2 * (qt + 1) - 1:
                    i = ecnt % 2
                    ecnt += 1
                    qs = slice(qt * TT, (qt + 1) * TT)
                    o_, r_, y_ = osb[i], rec[i], yb[i]
                    sc.op("act", [pok], [f"aO{i}"], lambda h: h.activation(out=o_[:], in_=po[0:65, :], func=AF.Identity))
                    pb, pbk = C.ps[6 + i], f"ps{6 + i}"
                    sc.op("pe", [f"aO{i}", "onesf"], [pbk], lambda h: h.matmul(pb[0:64, :], C.onesf[64:65, 0:64], o_[64:65, :], start=True, stop=True))
                    sc.op("dve", [pbk], [f"aR{i}"], lambda h: h.reciprocal(out=r_[:], in_=pb[0:64, :]))
                    sc.op("dve", [f"aR{i}", f"aO{i}"], [f"aY{i}"], lambda h: h.tensor_tensor(out=y_[:], in0=o_[0:64, :], in1=r_[:], op=ALU.mult))
                    sc.dma("sp", S["Y"][H["yrow"]:H["yrow"] + 64, qs], y_[:], [f"aY{i}"], ["Y"])

            prev = None
            for (qt, kp) in items:
                emit_qk_exp(qt, kp, gidx)
                if prev is not None:
                    emit_pv(*prev)
                prev = (qt, kp, gidx)
                gidx += 1
            emit_pv(*prev)
    sc.barrier()


def go(C, L, n):
    return L * 100 + n <= C.limit


def layer(C, I, S, L):
    sc, nc = C.sc, C.nc
    V = C.vecs
    if not go(C, L, 1):
        return
    with ExitStack() as ph:
        C.ph = ph
        rmsnorm_fm(C, S["H"], 0, 8, V, ["vecs"], S["HN"], 0, BF16, 1024, gcol0=0)
    sc.barrier()
    if not go(C, L, 2):
        return
    with ExitStack() as ph:
        C.ph = ph
        def dst_of(j):
            if j >= 15:
                return (S["PF32"], (j - 15) * 128 - j * 128, F32)
            return (S["PF"], 0, BF16)
        ev = store_evac(C, lambda j: ((S["PF32"], (j - 15) * 128, F32) if j >= 15 else (S["PF"], j * 128, BF16)))
        linear_fm(C, S["HN"], 0, 8, I["w_in_fm"][L], NIN_FM, 2048, ev)
    sc.barrier()
    if not go(C, L, 3):
        return
    with ExitStack() as ph:
        C.ph = ph
        linear_tm(C, S["HN"], 8, I["w_in_tm"][L], 512, S["PTM"])
    sc.barrier()
    if not go(C, L, 4):
        return
    with ExitStack() as ph:
        C.ph = ph
        rmsnorm_fm(C, S["PF"], 0, 2, V, ["vecs"], S["CQN"], 0, BF16, 256, gcol0=32, tag="rq")
    sc.barrier()
    with ExitStack() as ph:
        C.ph = ph
        rmsnorm_fm(C, S["PF"], 256, 1, V, ["vecs"], S["KVN"], 0, BF16, 128, gcol0=34, tag="rk")
    sc.barrier()
    if not go(C, L, 6):
        return
    mla_proj(C, I, S, L)
    with ExitStack() as ph:
        C.ph = ph
        linear_tm(C, S["KVN"], 1, I["w_uv"][L], 256, S["VM"])
    sc.barrier()
    if not go(C, L, 8):
        return
    attention_multi(C, S, [dict(q=[(S["QA"][h, 0:96, :], 0, 96)], k=[(S["KA"][h, 0:96, :], 0, 96)], kones=None, Cq=96,
                                v=S["VM"], vcol=64 * h, bias_head=None, scale=96 ** -0.5, yrow=64 * h) for h in range(4)], False)
    if not go(C, L, 9):
        return
    fox_prep(C, I, S, L)
    if not go(C, L, 10):
        return
    attention_multi(C, S, [dict(q=[(S["PF"][640 + 64 * h:640 + 64 * h + 64, :], 0, 64), (S["FAUG"][h], 64, 3)],
                                k=[(S["PF"][896 + 64 * h:896 + 64 * h + 64, :], 0, 64)], kones=(64, 3), Cq=67,
                                v=S["PTM"], vcol=64 * h, bias_head=h, scale=0.125, yrow=256 + 64 * h) for h in range(4)], True)
    if not go(C, L, 11):
        return
    hgrn2(C, I, S, L)
    if not go(C, L, 12):
        return
    s5(C, I, S, L)
    if not go(C, L, 13):
        return
    with ExitStack() as ph:
        C.ph = ph
        sg = [psb(C, f"gsg{i}", [128, 2, TT], BF16) for i in range(2)]
        def post_c(o, ko, t):
            i = t % 2
            sc.dma("pool", sg[i][:], S["PF"][1408:1664, t * TT:(t + 1) * TT].rearrange("(k p) n -> p k n", p=128), ["PF"], [f"gsg{i}"])
            sc.op("act", [f"gsg{i}"], [f"gsg{i}"], lambda h: h.activation(out=sg[i][:], in_=sg[i][:], func=AF.Sigmoid))
            sc.op("dve", [f"gsg{i}", ko], [ko], lambda h: h.tensor_tensor(out=o[:], in0=o[:], in1=sg[i][:], op=ALU.mult))
        for g in range(4):
            rmsnorm_fm(C, S["Y"], 256 * g, 2, V, ["vecs"], S["YN"], 256 * g, BF16, 256, gcol0=24 + 2 * g,
                       post=(post_c if g == 2 else None), tag=f"gn{g}")
    sc.barrier()
    if not go(C, L, 14):
        return
    with ExitStack() as ph:
        C.ph = ph
        linear_fm(C, S["YN"], 0, 8, I["w_out"][L], 8, 2048, resid_evac(C, S))
    sc.barrier()
    if not go(C, L, 15):
        return
    with ExitStack() as ph:
        C.ph = ph
        rmsnorm_fm(C, S["H"], 0, 8, V, ["vecs"], S["HN"], 0, BF16, 1024, gcol0=8)
    sc.barrier()
    ffn_up(C, I, S, L)
    if not go(C, L, 17):
        return
    with ExitStack() as ph:
        C.ph = ph
        linear_fm(C, S["ACT"], 0, 22, I["w_down"][L], 8, 2048, resid_evac(C, S))
    sc.barrier()
    if not go(C, L, 18):
        return
    with ExitStack() as ph:
        C.ph = ph
        rmsnorm_fm(C, S["H"], 0, 8, V, ["vecs"], S["HN"], 0, BF16, 1024, gcol0=16)
    sc.barrier()
    ple(C, I, S, L)


def linear_tm(C, src, KT, W, N, dst):
    sc, nc = C.sc, C.nc
    w = psb(C, "ltw", [128, KT, N], BF16)
    sc.dma("pool", w[:], W.rearrange("(k p) n -> p k n", p=128), [], ["ltw"])
    a = [psb(C, f"lta{i}", [128, KT, TT], BF16) for i in range(2)]
    o = [psb(C, f"lto{i}", [128, N], BF16) for i in range(3)]
    cnt = 0
    for t in range(NTT):
        i = t % 2
        sv = src[0:KT * 128, t * TT:(t + 1) * TT].rearrange("(k p) n -> p k n", p=128)
        sc.dma("pool", a[i][:], sv, ["HN", "KVN"], [f"lta{i}"])
        for b in range(4):
            pi = cnt % 4
            oi = cnt % 3
            cnt += 1
            pst, pk = C.ps[pi], f"ps{pi}"
            def mm(h, b=b, pst=pst, i=i):
                for k in range(KT):
                    ins = h.matmul(pst[:, 0:N], a[i][:, k, b * 128:(b + 1) * 128], w[:, k, :], start=(k == 0), stop=(k == KT - 1))
                return ins
            sc.op("pe", [f"lta{i}", "ltw"], [pk], mm)
            if cnt % 2 == 0:
                sc.op("act", [pk], [f"lto{oi}"], lambda h, pst=pst, oi=oi: h.activation(out=o[oi][:], in_=pst[:, 0:N], func=AF.Identity))
            else:
                sc.op("dve", [pk], [f"lto{oi}"], lambda h, pst=pst, oi=oi: h.tensor_copy(out=o[oi][:], in_=pst[:, 0:N]))
            r0 = t * TT + b * 128
            sc.dma("sp", dst[r0:r0 + 128, :], o[oi][:], [f"lto{oi}"], ["PTM", "VM"])


def mla_proj(C, I, S, L):
    sc, nc = C.sc, C.nc
    with ExitStack() as ph:
        C.ph = ph
        wq = psb(C, "mwq", [128, 2, 1024], BF16)
        wk = psb(C, "mwk", [128, 512], BF16)
        sc.dma("pool", wq[:], I["w_uq"][L].rearrange("(k p) n -> p k n", p=128), [], ["mwq"])
        sc.dma("pool", wk[:], I["w_uk"][L], [], ["mwk"])
        cq = [psb(C, f"mcq{i}", [128, 2, TT], BF16) for i in range(2)]
        kv = [psb(C, f"mkv{i}", [128, TT], BF16) for i in range(2)]
        kr = [psb(C, f"mkr{i}", [128, 2, TT], BF16) for i in range(2)]
        cc = [psb(C, f"mcc{i}", [128, TT]) for i in range(2)]
        ss = [psb(C, f"mss{i}", [128, TT]) for i in range(2)]
        stq = [psb(C, f"msq{i}", [128, TT], BF16) for i in range(3)]
        t1 = [psb(C, f"mt1{i}", [128, TT]) for i in range(2)]
        t2 = [psb(C, f"mt2{i}", [128, TT]) for i in range(2)]
        kpe = [psb(C, f"mkp{i}", [128, TT], BF16) for i in range(2)]
        cnt = 0
        for t in range(NTT):
            i = t % 2
            cs = slice(t * TT, (t + 1) * TT)
            sc.dma("pool", cq[i][:], S["CQN"][:, cs].rearrange("(k p) n -> p k n", p=128), ["CQN"], [f"mcq{i}"])
            sc.dma("pool", kv[i][:], S["KVN"][:, cs], ["KVN"], [f"mkv{i}"])
            sc.dma("pool", kr[i][:], S["PF"][384:640, cs].rearrange("(k p) n -> p k n", p=128), ["PF"], [f"mkr{i}"])
            sc.dma("pool", cc[i][64:96, :], S["ROPE"][0, :, cs], ["ROPE"], [f"mcc{i}"])
            sc.dma("pool", ss[i][64:96, :], S["ROPE"][1, :, cs], ["ROPE"], [f"mss{i}"])
            R = slice(64, 96)
            sc.op("dve", [f"mkr{i}", f"mcc{i}"], [f"mt1{i}"], lambda h: h.tensor_tensor(out=t1[i][R, :], in0=kr[i][R, 0, :], in1=cc[i][R, :], op=ALU.mult))
            sc.op("dve", [f"mkr{i}", f"mss{i}"], [f"mt2{i}"], lambda h: h.tensor_tensor(out=t2[i][R, :], in0=kr[i][R, 1, :], in1=ss[i][R, :], op=ALU.mult))
            sc.op("dve", [f"mt1{i}", f"mt2{i}"], [f"mkp{i}"], lambda h: h.tensor_tensor(out=kpe[i][R, :], in0=t1[i][R, :], in1=t2[i][R, :], op=ALU.add))
            for hd in range(4):
                sc.dma("sp", S["KA"][hd, 64:96, cs], kpe[i][R, :], [f"mkp{i}"], ["KA"])
            for hd in range(4):
                pi = cnt % 4
                si = cnt % 3
                cnt += 1
                pst, pk = C.ps[pi], f"ps{pi}"
                sc.op("pe", [f"mkv{i}", "mwk"], [pk], lambda h, pst=pst, hd=hd: h.matmul(pst[:], wk[:, hd * 128:(hd + 1) * 128], kv[i][:], start=True, stop=True))
                sc.op("act", [pk], [f"msq{si}"], lambda h, pst=pst, si=si: h.activation(out=stq[si][0:64, :], in_=pst[0:64, :], func=AF.Identity))
                sc.dma("sp", S["KA"][hd, 0:64, cs], stq[si][0:64, :], [f"msq{si}"], ["KA"])
                pa, pb_ = cnt % 4, (cnt + 1) % 4
                si = cnt % 3
                cnt += 2
                psA, psB = C.ps[pa], C.ps[pb_]
                def mmq(h, hd=hd, psA=psA, psB=psB):
                    for which, pst in ((0, psA), (1, psB)):
                        c0 = (2 * hd + which) * 128
                        for k in range(2):
                            ins = h.matmul(pst[:], wq[:, k, c0:c0 + 128], cq[i][:, k, :], start=(k == 0), stop=(k == 1))
                    return ins
                sc.op("pe", [f"mcq{i}", "mwq"], [f"ps{pa}", f"ps{pb_}"], mmq)
                sq_ = stq[si]
                sc.op("act", [f"ps{pa}"], [f"msq{si}"], lambda h, psA=psA, sq_=sq_: h.activation(out=sq_[0:64, :], in_=psA[0:64, :], func=AF.Identity))
                sc.op("dve", [f"ps{pa}", f"mcc{i}"], [f"mt1{i}"], lambda h, psA=psA: h.tensor_tensor(out=t1[i][R, :], in0=psA[R, :], in1=cc[i][R, :], op=ALU.mult))
                sc.op("dve", [f"ps{pb_}", f"mss{i}"], [f"mt2{i}"], lambda h, psB=psB: h.tensor_tensor(out=t2[i][R, :], in0=psB[R, :], in1=ss[i][R, :], op=ALU.mult))
                sc.op("dve", [f"mt1{i}", f"mt2{i}"], [f"msq{si}"], lambda h, sq_=sq_: h.tensor_tensor(out=sq_[R, :], in0=t1[i][R, :], in1=t2[i][R, :], op=ALU.add))
                sc.dma("sp", S["QA"][hd, 0:96, cs], sq_[0:96, :], [f"msq{si}"], ["QA"])
    sc.barrier()


def fox_prep(C, I, S, L):
    sc, nc = C.sc, C.nc
    with ExitStack() as ph:
        C.ph = ph
        z = psb(C, "fz", [4, NT])
        cum = psb(C, "fcum", [4, NT])
        hi = psb(C, "fhi", [4, NT], BF16)
        mid = psb(C, "fmid", [4, NT], BF16)
        lo = psb(C, "flo", [4, NT], BF16)
        r1 = psb(C, "fr1", [4, NT])
        bt = psb(C, "fbt", [128, 256])
        sc.dma("sp", z[:], S["PF32"][256:260, :], ["PF32"], ["fz"])
        sc.op("act", ["fz", "vecs"], ["fz"], lambda h: h.activation(out=z[:], in_=z[:], func=AF.Sigmoid, bias=C.vecs[0:4, 35:36]))
        sc.op("act", ["fz"], ["fz"], lambda h: h.activation(out=z[:], in_=z[:], func=AF.Ln))
        for g in range(4):
            cs = slice(g * 2048, (g + 1) * 2048)
            init = 0.0 if g == 0 else cum[:, g * 2048 - 1:g * 2048]
            sc.op("dve", ["fz", "onesf", "fcum"], ["fcum"], lambda h, cs=cs, init=init: h.tensor_tensor_scan(
                out=cum[:, cs], data0=C.onesf[0:4, :], data1=z[:, cs], initial=init, op0=ALU.mult, op1=ALU.add))
        pst = C.ps[0]
        def tr(h):
            for b in range(64):
                ins = h.transpose(pst[:, 4 * b:4 * b + 4], cum[0:4, b * 128:(b + 1) * 128], C.ident[0:4, 0:4])
            return ins
        sc.op("pe", ["fcum", "ident"], ["ps0"], tr)
        sc.op("act", ["ps0"], ["fbt"], lambda h: h.activation(out=bt[:], in_=pst[:, 0:256], func=AF.Identity, scale=-1.0))
        sc.dma("sp", S["FBIAS"].rearrange("p b h -> p (b h)"), bt[:], ["fbt"], ["FBIAS"])
        sc.op("dve", ["fcum"], ["fr1"], lambda h: h.tensor_scalar(out=r1[:], in0=cum[:], scalar1=8.0, scalar2=None, op0=ALU.mult))
        sc.op("dve", ["fr1"], ["fhi"], lambda h: h.tensor_copy(out=hi[:], in_=r1[:]))
        sc.op("dve", ["fr1", "fhi"], ["fr1"], lambda h: h.tensor_tensor(out=r1[:], in0=r1[:], in1=hi[:], op=ALU.subtract))
        sc.op("dve", ["fr1"], ["fmid"], lambda h: h.tensor_copy(out=mid[:], in_=r1[:]))
        sc.op("dve", ["fr1", "fmid"], ["fr1"], lambda h: h.tensor_tensor(out=r1[:], in0=r1[:], in1=mid[:], op=ALU.subtract))
        sc.op("dve", ["fr1"], ["flo"], lambda h: h.tensor_copy(out=lo[:], in_=r1[:]))
        for hd in range(4):
            for j, tl, k in ((0, hi, "fhi"), (1, mid, "fmid"), (2, lo, "flo")):
                sc.dma("sp", S["FAUG"][hd, j:j + 1, :], tl[hd:hd + 1, :], [k], ["FAUG"])
    sc.barrier()


def hgrn2(C, I, S, L):
    sc, nc = C.sc, C.nc
    SCW = 2048
    CH = 64
    NCH = SCW // CH
    MID, LAST = CH // 2 - 1, CH - 1
    with ExitStack() as ph:
        C.ph = ph
        lbt = psb(C, "hlb", [128, 2, 2])
        lb = psb(C, "hlbv", [128, 2])
        oml = psb(C, "homl", [128, 2])
        rm = psb(C, "hrm", [128, SCW])
        sc.dma("sp", lbt[:], I["lbp"], [], ["hlb"])
        if L == 0:
            sc.op("dve", [], ["hlbv"], lambda h: h.memset(lb[:], 0.0))
        else:
            sc.op("dve", ["hlb"], ["hlbv"], lambda h: h.tensor_tensor(out=lb[:], in0=lbt[:, :, 1], in1=lbt[:, :, 0], op=ALU.subtract))
            sc.op("act", ["hlbv"], ["hlbv"], lambda h: h.activation(out=lb[:], in_=lb[:], func=AF.Sigmoid))
        sc.op("dve", ["hlbv"], ["homl"], lambda h: h.tensor_scalar(out=oml[:], in0=lb[:], scalar1=-1.0, scalar2=1.0, op0=ALU.mult, op1=ALU.add))
        sc.op("dve", [], ["hrm"], lambda h: h.memset(rm[:], 1.0))
        sc.op("dve", ["hrm"], ["hrm"], lambda h: h.memset(rm[:].rearrange("p (c j) -> p c j", j=CH)[:, :, 0:1], 0.0))
        z = psb(C, "hz", [128, SCW]); f = psb(C, "hf", [128, SCW]); kkf = psb(C, "hkk", [128, SCW])
        b = psb(C, "hb", [128, SCW]); nb = psb(C, "hnb", [128, SCW]); e1 = psb(C, "he1", [128, SCW])
        q = psb(C, "hq", [128, SCW], BF16)
        qa = psb(C, "hqa", [128, SCW], BF16); ka = psb(C, "hka", [128, SCW], BF16)
        qt_ = psb(C, "hqt", [128, SCW], BF16); kl = psb(C, "hkl", [128, SCW], BF16)
        dd = psb(C, "hd", [128, NCH])
        it = psb(C, "hit", [CH, NCH, 128], BF16)
        klT = [psb(C, f"hklT{i}", [CH, 128], BF16) for i in range(2)]
        scb = [psb(C, f"hsc{i}", [CH, CH], BF16) for i in range(2)]
        Sf = [psb(C, f"hSf{i}", [128, 64]) for i in range(2)]
        Sb = [psb(C, f"hSb{i}", [128, 64], BF16) for i in range(2)]
        tmp = [psb(C, f"htmp{i}", [128, 64]) for i in range(2)]
        ys = [psb(C, f"hys{i}", [64, SCW], BF16) for i in range(2)]
        for pair in range(2):
            for hh in range(2):
                sc.op("dve", [], [f"hSf{hh}"], lambda h, hh=hh: h.memset(Sf[hh][:], 0.0))
                sc.op("dve", [], [f"hSb{hh}"], lambda h, hh=hh: h.memset(Sb[hh][:], 0.0))
            for s_ in range(NT // SCW):
                cs = slice(s_ * SCW, (s_ + 1) * SCW)
                sc.dma("pool", z[:], S["PF32"][128 * pair:128 * pair + 128, cs], ["PF32"], ["hz"])
                sc.dma("pool", q[:], S["PF"][1152 + 128 * pair:1152 + 128 * pair + 128, cs], ["PF"], ["hq"])
                sc.dma("pool", it[:], S["PTM"][cs, 256 + 128 * pair:256 + 128 * pair + 128].rearrange("(c p) d -> p c d", p=CH), ["PTM"], ["hit"])
                sc.op("act", ["hz"], ["hz"], lambda h: h.activation(out=z[:], in_=z[:], func=AF.Sigmoid))
                sc.op("dve", ["hz", "hlbv", "homl"], ["hf"], lambda h: h.tensor_scalar(out=f[:], in0=z[:], scalar1=oml[:, pair:pair + 1], scalar2=lb[:, pair:pair + 1], op0=ALU.mult, op1=ALU.add))
                sc.op("dve", ["hf"], ["hkk"], lambda h: h.tensor_scalar(out=kkf[:], in0=f[:], scalar1=-1.0, scalar2=1.0, op0=ALU.mult, op1=ALU.add))
                sc.op("act", ["hf"], ["hf"], lambda h: h.activation(out=f[:], in_=f[:], func=AF.Ln))
                sc.op("dve", ["hf", "hrm"], ["hb"], lambda h: h.tensor_tensor_scan(out=b[:], data0=rm[:], data1=f[:], initial=0.0, op0=ALU.mult, op1=ALU.add))
                sc.op("dve", ["hb"], ["hnb"], lambda h: h.tensor_scalar(out=nb[:], in0=b[:], scalar1=-1.0, scalar2=None, op0=ALU.mult))
                b3 = b[:].rearrange("p (c j) -> p c j", j=CH)
                sc.op("act", ["hb"], ["hd"], lambda h: h.activation(out=dd[:], in_=b3[:, :, LAST], func=AF.Exp))
                sc.op("act", ["hb"], ["he1"], lambda h: h.activation(out=e1[:], in_=b[:], func=AF.Exp))
                sc.op("dve", ["he1", "hq"], ["hqt"], lambda h: h.tensor_tensor(out=qt_[:], in0=q[:], in1=e1[:], op=ALU.mult))
                for c in range(NCH):
                    cc_ = slice(c * CH, (c + 1) * CH)
                    sc.op("act", ["hb", "hnb", "hqt"], ["he1"], lambda h, c=c, cc_=cc_: h.activation(out=e1[:, cc_], in_=b[:, cc_], func=AF.Exp, bias=nb[:, c * CH + MID:c * CH + MID + 1]))
                sc.op("dve", ["he1", "hq"], ["hqa"], lambda h: h.tensor_tensor(out=qa[:], in0=q[:], in1=e1[:], op=ALU.mult))
                for c in range(NCH):
                    cc_ = slice(c * CH, (c + 1) * CH)
                    sc.op("act", ["hb", "hqa"], ["he1"], lambda h, c=c, cc_=cc_: h.activation(out=e1[:, cc_], in_=b[:, cc_], func=AF.Exp, scale=-1.0, bias=b[:, c * CH + MID:c * CH + MID + 1]))
                sc.op("dve", ["he1", "hkk"], ["hka"], lambda h: h.tensor_tensor(out=ka[:], in0=kkf[:], in1=e1[:], op=ALU.mult))
                for c in range(NCH):
                    cc_ = slice(c * CH, (c + 1) * CH)
                    sc.op("act", ["hb", "hka"], ["he1"], lambda h, c=c, cc_=cc_: h.activation(out=e1[:, cc_], in_=b[:, cc_], func=AF.Exp, scale=-1.0, bias=b[:, c * CH + LAST:c * CH + LAST + 1]))
                sc.op("dve", ["he1", "hkk"], ["hkl"], lambda h: h.tensor_tensor(out=kl[:], in0=kkf[:], in1=e1[:], op=ALU.mult))
                for c in range(NCH):
                    cc_ = slice(c * CH, (c + 1) * CH)
                    ti = c % 2
                    pstT, pkT = C.ps[6], "ps6"
                    sc.op("pe", ["hkl", "identb"], [pkT], lambda h, cc_=cc_: h.matmul(pstT[0:CH, 0:128], kl[:, cc_], C.identb[:], start=True, stop=True))
                    sc.op("act", [pkT], [f"hklT{ti}"], lambda h, ti=ti: h.activation(out=klT[ti][:], in_=pstT[0:CH, 0:128], func=AF.Identity))
                    for hh in range(2):
                        Pr = slice(64 * hh, 64 * hh + 64)
                        hcol = slice(64 * hh, 64 * hh + 64)
                        pS_, pSk = C.ps[hh], f"ps{hh}"
                        sc.op("pe", ["hka", "hqa"], [pSk], lambda h, Pr=Pr, cc_=cc_, pS_=pS_: h.matmul(pS_[0:CH, 0:CH], ka[Pr, cc_], qa[Pr, cc_], start=True, stop=True))
                        sc.op("dve", [pSk, "amask"], [f"hsc{hh}"], lambda h, hh=hh, pS_=pS_: h.scalar_tensor_tensor(
                            out=scb[hh][:], in0=pS_[0:CH, 0:CH], scalar=1e30, in1=C.amask[0:CH, 0, 0:CH], op0=ALU.min, op1=ALU.mult))
                        po, pok = C.ps[2 + hh], f"ps{2 + hh}"
                        def mo(h, hh=hh, c=c, cc_=cc_, hcol=hcol, po=po):
                            h.matmul(po[0:64, 0:CH], it[:, c, hcol], scb[hh][:], start=True, stop=False)
                            return h.matmul(po[0:64, 0:CH], Sb[hh][:], qt_[:, cc_], start=False, stop=True)
                        sc.op("pe", [f"hsc{hh}", "hit", f"hSb{hh}", "hqt"], [pok], mo)
                        sc.op("act", [pok], [f"hys{hh}"], lambda h, hh=hh, cc_=cc_, po=po: h.activation(out=ys[hh][:, cc_], in_=po[0:64, 0:CH], func=AF.Identity))
                        pd, pdk = C.ps[4 + hh], f"ps{4 + hh}"
                        sc.op("pe", [f"hklT{ti}", "hit"], [pdk], lambda h, ti=ti, c=c, hcol=hcol, pd=pd: h.matmul(pd[:, 0:64], klT[ti][:], it[:, c, hcol], start=True, stop=True))
                        sc.op("dve", [f"hSf{hh}", "hd"], [f"htmp{hh}"], lambda h, hh=hh, c=c: h.tensor_scalar(out=tmp[hh][:], in0=Sf[hh][:], scalar1=dd[:, c:c + 1], scalar2=None, op0=ALU.mult))
                        sc.op("dve", [pdk, f"htmp{hh}", "cvec"], [f"hSf{hh}"], lambda h, hh=hh, pd=pd: h.scalar_tensor_tensor(
                            out=Sf[hh][:], in0=pd[:, 0:64], scalar=C.cvec[:, 2 + 2 * hh:3 + 2 * hh], in1=tmp[hh][:], op0=ALU.mult, op1=ALU.add))
                        sc.op("act", [f"hSf{hh}"], [f"hSb{hh}"], lambda h, hh=hh: h.activation(out=Sb[hh][:], in_=Sf[hh][:], func=AF.Identity))
                for hh in range(2):
                    r0 = 512 + 128 * pair + 64 * hh
                    sc.dma("sp", S["Y"][r0:r0 + 64, cs], ys[hh][:], [f"hys{hh}"], ["Y"])
    sc.barrier()


S5_SHIFTS = [(1, 4), (4, 4), (16, 4), (64, 4), (256, 4), (1024, 4), (4096, 2)]


def s5(C, I, S, L):
    sc, nc = C.sc, C.nc
    with ExitStack() as ph:
        C.ph = ph
        pa = psb(C, "s5a", [128, 3, 16])
        sc.dma("sp", pa[:], I["s5a"][L], [], ["s5a"])
        stp = psb(C, "s5stp", [128, 16]); lre = psb(C, "s5lre", [128, 16]); mag = psb(C, "s5mag", [128, 16])
        ang = psb(C, "s5ang", [128, 16]); ki = psb(C, "s5ki", [128, 16], I32); kf = psb(C, "s5kf", [128, 16])
        sn = psb(C, "s5sn", [128, 16]); cs_ = psb(C, "s5cs", [128, 16])
        NP = 19
        pre = psb(C, "s5pre", [128, NP + 1, 16]); pim = psb(C, "s5pim", [128, NP + 1, 16])
        t1 = psb(C, "s5t1", [128, 16]); t2 = psb(C, "s5t2", [128, 16])
        v1 = psb(C, "s5v1", [128, NP + 1, 16]); v2 = psb(C, "s5v2", [128, NP + 1, 16])
        K = ["s5p"]
        def dv(fn, r=K, w=K):
            sc.op("dve", r, w, fn)
        def av(fn, r=K, w=K):
            sc.op("act", r, w, fn)
        sc.op("act", ["s5a"], K, lambda h: h.activation(out=stp[:], in_=pa[:, 2, :], func=AF.Exp))
        dv(lambda h: h.tensor_scalar(out=lre[:], in0=pa[:, 0, :], scalar1=-1e-4, scalar2=None, op0=ALU.min), r=K + ["s5a"])
        dv(lambda h: h.tensor_tensor(out=t1[:], in0=lre[:], in1=stp[:], op=ALU.mult))
        av(lambda h: h.activation(out=mag[:], in_=t1[:], func=AF.Exp))
        dv(lambda h: h.tensor_tensor(out=ang[:], in0=pa[:, 1, :], in1=stp[:], op=ALU.mult), r=K + ["s5a"])
        def sincos(dst, shift):
            dv(lambda h: h.tensor_scalar(out=t1[:], in0=ang[:], scalar1=shift, scalar2=None, op0=ALU.add))
            dv(lambda h: h.tensor_scalar(out=ki[:], in0=t1[:], scalar1=1.0 / TWO_PI, scalar2=None, op0=ALU.mult))
            dv(lambda h: h.tensor_copy(out=kf[:], in_=ki[:]))
            dv(lambda h: h.scalar_tensor_tensor(out=t1[:], in0=kf[:], scalar=-TWO_PI, in1=t1[:], op0=ALU.mult, op1=ALU.add))
            dv(lambda h: h.tensor_scalar(out=t1[:], in0=t1[:], scalar1=3.1415925, scalar2=-3.1415925, op0=ALU.min, op1=ALU.max))
            av(lambda h: h.activation(out=dst[:], in_=t1[:], func=AF.Sin))
        sincos(sn, 0.0)
        sincos(cs_, math.pi / 2)
        dv(lambda h: h.tensor_tensor(out=pre[:, 0, :], in0=mag[:], in1=cs_[:], op=ALU.mult))
        dv(lambda h: h.tensor_tensor(out=pim[:, 0, :], in0=mag[:], in1=sn[:], op=ALU.mult))
        exps = []
        for s_, r_ in S5_SHIFTS:
            for m in range(1, r_):
                exps.append(s_ * m)
        idx = {1: 0}
        def cmul(dst, a, b_):
            dv(lambda h: h.tensor_tensor(out=t1[:], in0=pre[:, a, :], in1=pre[:, b_, :], op=ALU.mult))
            dv(lambda h: h.tensor_tensor(out=t2[:], in0=pim[:, a, :], in1=pim[:, b_, :], op=ALU.mult))
            dv(lambda h: h.tensor_tensor(out=pre[:, dst, :], in0=t1[:], in1=t2[:], op=ALU.subtract))
            dv(lambda h: h.tensor_tensor(out=t1[:], in0=pre[:, a, :], in1=pim[:, b_, :], op=ALU.mult))
            dv(lambda h: h.tensor_tensor(out=t2[:], in0=pim[:, a, :], in1=pre[:, b_, :], op=ALU.mult))
            dv(lambda h: h.tensor_tensor(out=pim[:, dst, :], in0=t1[:], in1=t2[:], op=ALU.add))
        nxt = 1
        for e in exps:
            if e in idx:
                continue
            lowbit = e & (-e)
            base = 1
            while base * 4 <= e:
                base *= 4
            if e == base and e != 1:
                cmul(nxt, idx[base // 2], idx[base // 2])
            elif e == 2 * base:
                cmul(nxt, idx[base], idx[base])
            else:
                cmul(nxt, idx[2 * base], idx[base])
            idx[e] = nxt
            nxt += 1
        assert nxt <= NP + 1, nxt
        dv(lambda h: h.tensor_scalar(out=v1[:], in0=pre[:], scalar1=C.cvec[:, 2:3], scalar2=None, op0=ALU.mult), r=K + ["cvec"])
        dv(lambda h: h.scalar_tensor_tensor(out=v1[:], in0=pim[:], scalar=C.cvec[:, 3:4], in1=v1[:], op0=ALU.mult, op1=ALU.add), r=K + ["cvec"])
        dv(lambda h: h.tensor_scalar(out=v2[:], in0=pim[:], scalar1=C.cvec[:, 2:3], scalar2=None, op0=ALU.mult), r=K + ["cvec"])
        dv(lambda h: h.scalar_tensor_tensor(out=v2[:], in0=pre[:], scalar=C.cvec[:, 4:5], in1=v2[:], op0=ALU.mult, op1=ALU.add), r=K + ["cvec"])
        E = psb(C, "s5E", [128, 64])
        dv(lambda h: h.tensor_tensor(out=E[:], in0=C.ident[:, 0:64], in1=C.ident[:, 64:128], op=ALU.add), r=["ident"], w=["s5E"])
        pb = psb(C, "s5b", [128, 2, 5, 64])
        sc.dma("sp", pb[:], I["s5b"][L], [], ["s5b"])
        KB = ["s5q"]
        def dvb(fn, r=KB, w=KB):
            sc.op("dve", r, w, fn)
        def avb(fn):
            sc.op("act", KB, KB, fn)
        sh = [128, 2, 64]
        bst = psb(C, "b_st", sh); blr = psb(C, "b_lr", sh); bmg = psb(C, "b_mg", sh); ban = psb(C, "b_an", sh)
        bt1 = psb(C, "b_t1", sh); bt2 = psb(C, "b_t2", sh); bki = psb(C, "b_ki", sh, I32); bkf = psb(C, "b_kf", sh)
        bsn = psb(C, "b_sn", sh); bcs = psb(C, "b_cs", sh); bar = psb(C, "b_ar", sh); bai = psb(C, "b_ai", sh)
        bden = psb(C, "b_den", sh); bcr = psb(C, "b_cr", sh); bci = psb(C, "b_ci", sh)
        bbT = psb(C, "b_bbT", [128, 2, 128])
        sc.op("act", ["s5b"], KB, lambda h: h.activation(out=bst[:], in_=pb[:, :, 2, :], func=AF.Exp))
        dvb(lambda h: h.tensor_scalar(out=blr[:], in0=pb[:, :, 0, :], scalar1=-1e-4, scalar2=None, op0=ALU.min), r=KB + ["s5b"])
        dvb(lambda h: h.tensor_tensor(out=bt1[:], in0=blr[:], in1=bst[:], op=ALU.mult))
        avb(lambda h: h.activation(out=bmg[:], in_=bt1[:], func=AF.Exp))
        dvb(lambda h: h.tensor_tensor(out=ban[:], in0=pb[:, :, 1, :], in1=bst[:], op=ALU.mult), r=KB + ["s5b"])
        def sincos_b(dst, shift):
            dvb(lambda h: h.tensor_scalar(out=bt1[:], in0=ban[:], scalar1=shift, scalar2=None, op0=ALU.add))
            dvb(lambda h: h.tensor_scalar(out=bki[:], in0=bt1[:], scalar1=1.0 / TWO_PI, scalar2=None, op0=ALU.mult))
            dvb(lambda h: h.tensor_copy(out=bkf[:], in_=bki[:]))
            dvb(lambda h: h.scalar_tensor_tensor(out=bt1[:], in0=bkf[:], scalar=-TWO_PI, in1=bt1[:], op0=ALU.mult, op1=ALU.add))
            dvb(lambda h: h.tensor_scalar(out=bt1[:], in0=bt1[:], scalar1=3.1415925, scalar2=-3.1415925, op0=ALU.min, op1=ALU.max))
            avb(lambda h: h.activation(out=dst[:], in_=bt1[:], func=AF.Sin))
        sincos_b(bsn, 0.0)
        sincos_b(bcs, math.pi / 2)
        dvb(lambda h: h.tensor_tensor(out=bar[:], in0=bmg[:], in1=bcs[:], op=ALU.mult))
        dvb(lambda h: h.tensor_scalar(out=bar[:], in0=bar[:], scalar1=-1.0, scalar2=None, op0=ALU.add))
        dvb(lambda h: h.tensor_tensor(out=bai[:], in0=bmg[:], in1=bsn[:], op=ALU.mult))
        lim = pb[:, :, 1, :]
        dvb(lambda h: h.tensor_tensor(out=bt1[:], in0=blr[:], in1=blr[:], op=ALU.mult))
        dvb(lambda h: h.tensor_tensor(out=bt2[:], in0=lim, in1=lim, op=ALU.mult), r=KB + ["s5b"])
        dvb(lambda h: h.tensor_tensor(out=bden[:], in0=bt1[:], in1=bt2[:], op=ALU.add))
        dvb(lambda h: h.reciprocal(out=bden[:], in_=bden[:]))
        dvb(lambda h: h.tensor_tensor(out=bt1[:], in0=bar[:], in1=blr[:], op=ALU.mult))
        dvb(lambda h: h.tensor_tensor(out=bt2[:], in0=bai[:], in1=lim, op=ALU.mult), r=KB + ["s5b"])
        dvb(lambda h: h.tensor_tensor(out=bt1[:], in0=bt1[:], in1=bt2[:], op=ALU.add))
        dvb(lambda h: h.tensor_tensor(out=bcr[:], in0=bt1[:], in1=bden[:], op=ALU.mult))
        dvb(lambda h: h.tensor_tensor(out=bt1[:], in0=bai[:], in1=blr[:], op=ALU.mult))
        dvb(lambda h: h.tensor_tensor(out=bt2[:], in0=bar[:], in1=lim, op=ALU.mult), r=KB + ["s5b"])
        dvb(lambda h: h.tensor_tensor(out=bt1[:], in0=bt1[:], in1=bt2[:], op=ALU.subtract))
        dvb(lambda h: h.tensor_tensor(out=bci[:], in0=bt1[:], in1=bden[:], op=ALU.mult))
        bre_, bim_ = pb[:, :, 3, :], pb[:, :, 4, :]
        dvb(lambda h: h.tensor_tensor(out=bt1[:], in0=bcr[:], in1=bre_, op=ALU.mult), r=KB + ["s5b"])
        dvb(lambda h: h.tensor_tensor(out=bt2[:], in0=bci[:], in1=bim_, op=ALU.mult), r=KB + ["s5b"])
        dvb(lambda h: h.tensor_tensor(out=bbT[:, :, 0:64], in0=bt1[:], in1=bt2[:], op=ALU.subtract))
        dvb(lambda h: h.tensor_tensor(out=bt1[:], in0=bcr[:], in1=bim_, op=ALU.mult), r=KB + ["s5b"])
        dvb(lambda h: h.tensor_tensor(out=bt2[:], in0=bci[:], in1=bre_, op=ALU.mult), r=KB + ["s5b"])
        dvb(lambda h: h.tensor_tensor(out=bbT[:, :, 64:128], in0=bt1[:], in1=bt2[:], op=ALU.add))
        pc = psb(C, "s5c", [128, 16, 16])
        cb = psb(C, "s5cb", [128, 16, 16], BF16)
        sc.dma("sp", pc[:], I["s5c"][L], [], ["s5c"])
        sc.op("dve", ["s5c", "cvec"], ["s5cb"], lambda h: h.tensor_scalar(out=cb[:], in0=pc[:], scalar1=C.cvec[:, 5:6], scalar2=None, op0=ALU.mult))
        u = psb(C, "s5u", [128, 2, NT], BF16)
        sc.dma("sp", u[:], S["PF"][1664:1920, :].rearrange("(k p) n -> p k n", p=128), ["PF"], ["s5u"])
        X = [psb(C, f"s5X{i}", [128, NT], BF16) for i in range(2)]
        Am = [psb(C, f"s5A{i}", [128, NP, 128], BF16) for i in range(2)]
        bw = [psb(C, f"s5bw{i}", [128, 128], BF16) for i in range(2)]
        yst = [psb(C, f"s5y{i}", [16, TT]) for i in range(3)]
        ecnt = 0
        for g in range(16):
            gi = g % 2
            A, Ak = Am[gi], f"s5A{gi}"
            for e, j in idx.items():
                sc.op("dve", K + ["s5E"], [Ak], lambda h, j=j: h.tensor_scalar(out=A[:, j, 0:64], in0=E[:], scalar1=v1[:, j, g:g + 1], scalar2=None, op0=ALU.mult))
                sc.op("dve", K + ["s5E"], [Ak], lambda h, j=j: h.tensor_scalar(out=A[:, j, 64:128], in0=E[:], scalar1=v2[:, j, g:g + 1], scalar2=None, op0=ALU.mult))
            sc.op("dve", KB + ["cvec"], [f"s5bw{gi}"], lambda h: h.tensor_scalar(out=bw[gi][:], in0=bbT[:, g // 8, :], scalar1=C.cvec[:, 8 + g % 8:9 + g % 8], scalar2=None, op0=ALU.mult))
            cur = 0
            for t in range(NTT):
                pi = ecnt % 4
                pst, pk = C.ps[pi], f"ps{pi}"
                sc.op("pe", [f"s5bw{gi}", "s5u"], [pk], lambda h, pst=pst, t=t: h.matmul(pst[:], bw[gi][:], u[:, g // 8, t * TT:(t + 1) * TT], start=True, stop=True))
                ecnt += 1
                xk = f"s5X{cur}t{t}"
                if ecnt % 2:
                    sc.op("act", [pk], [xk], lambda h, pst=pst, t=t: h.activation(out=X[cur][:, t * TT:(t + 1) * TT], in_=pst[:], func=AF.Identity))
                else:
                    sc.op("dve", [pk], [xk], lambda h, pst=pst, t=t: h.tensor_copy(out=X[cur][:, t * TT:(t + 1) * TT], in_=pst[:]))
            for s_, r_ in S5_SHIFTS:
                nx = 1 - cur
                for t in range(NTT):
                    pi = ecnt % 4
                    ecnt += 1
                    pst, pk = C.ps[pi], f"ps{pi}"
                    t0 = t * TT
                    rk = {f"s5X{cur}t{t}"}
                    terms = [m for m in range(1, r_) if m * s_ < t0 + TT]
                    fast = len(terms) > 0 and all(m * s_ <= t0 for m in terms) and (t % 4 != 3)
                    def mm(h, t0=t0, pst=pst, s_=s_, r_=r_, cur=cur, terms=terms, fast=fast):
                        ins = None
                        if not fast:
                            ins = h.matmul(pst[:], C.identb[:], X[cur][:, t0:t0 + TT], start=True, stop=(len(terms) == 0))
                        for ii, m in enumerate(terms):
                            sh_ = m * s_
                            off = max(0, sh_ - t0)
                            ins = h.matmul(pst[:, off:TT], A[:, idx[sh_], :], X[cur][:, t0 + off - sh_:t0 + TT - sh_], start=(fast and ii == 0), stop=(ii == len(terms) - 1))
                        return ins
                    for m in range(1, r_):
                        sh_ = m * s_
                        if sh_ < t0 + TT:
                            lo_t = max(0, (t0 - sh_)) // TT
                            hi_t = (t0 + TT - 1 - sh_) // TT
                            for tt_ in range(lo_t, hi_t + 1):
                                rk.add(f"s5X{cur}t{tt_}")
                    sc.op("pe", sorted(rk) + [Ak, "identb"], [pk], mm)
                    xk = f"s5X{nx}t{t}"
                    if fast:
                        sc.op("dve", [pk, f"s5X{cur}t{t}"], [xk], lambda h, pst=pst, t0=t0, nx=nx, cur=cur: h.tensor_tensor(
                            out=X[nx][:, t0:t0 + TT], in0=pst[:], in1=X[cur][:, t0:t0 + TT], op=ALU.add))
                    else:
                        sc.op("act", [pk], [xk], lambda h, pst=pst, t0=t0, nx=nx: h.activation(out=X[nx][:, t0:t0 + TT], in_=pst[:], func=AF.Identity))
                cur = nx
            for t in range(NTT):
                pi = ecnt % 4
                yi = ecnt % 3
                ecnt += 1
                pst, pk = C.ps[pi], f"ps{pi}"
                sc.op("pe", [f"s5X{cur}t{t}", "s5cb"], [pk], lambda h, pst=pst, t=t, cur=cur: h.matmul(pst[0:16, :], cb[:, g, :], X[cur][:, t * TT:(t + 1) * TT], start=True, stop=True))
                sc.op("act", [pk], [f"s5y{yi}"], lambda h, pst=pst, yi=yi: h.activation(out=yst[yi][:], in_=pst[0:16, :], func=AF.Identity))
                sc.dma("sp", S["YS5"][16 * g:16 * g + 16, t * TT:(t + 1) * TT], yst[yi][:], [f"s5y{yi}"], ["YS5"])
    sc.barrier()
    with ExitStack() as ph:
        C.ph = ph
        V = C.vecs
        wg = psb(C, "gwg", [128, 2, 256], BF16)
        sc.dma("pool", wg[:], I["w_glu"][L].rearrange("(k p) n -> p k n", p=128), [], ["gwg"])
        yy = [psb(C, f"gy{i}", [128, 2, TT]) for i in range(2)]
        uu = [psb(C, f"gu{i}", [128, 2, TT], BF16) for i in range(2)]
        x2 = [psb(C, f"gx2{i}", [128, 2, TT]) for i in range(2)]
        za = [psb(C, f"gza{i}", [128, 2, TT]) for i in range(2)]
        zb = [psb(C, f"gzb{i}", [128, 2, TT], BF16) for i in range(2)]
        sg = [psb(C, f"gsg{i}", [128, TT]) for i in range(2)]
        ob = [psb(C, f"gob{i}", [128, 2, TT], BF16) for i in range(2)]
        for t in range(NTT):
            i = t % 2
            cs = slice(t * TT, (t + 1) * TT)
            ky, ku, kx, kz, kzb, ko = f"gy{i}", f"gu{i}", f"gx2{i}", f"gza{i}", f"gzb{i}", f"gob{i}"
            sc.dma("pool", yy[i][:], S["YS5"][:, cs].rearrange("(k p) n -> p k n", p=128), ["YS5"], [ky])
            sc.dma("pool", uu[i][:], S["PF"][1664:1920, cs].rearrange("(k p) n -> p k n", p=128), ["PF"], [ku])
            for k in range(2):
                sc.op("dve", [ky, ku, "vecs"], [ky], lambda h, k=k: h.scalar_tensor_tensor(out=yy[i][:, k, :], in0=uu[i][:, k, :], scalar=V[:, 36 + k:37 + k], in1=yy[i][:, k, :], op0=ALU.mult, op1=ALU.add))
            sc.op("act", [ky], [kx], lambda h: h.activation(out=x2[i][:], in_=yy[i][:], func=AF.Square))
            sc.op("dve", [kx], [kx], lambda h: h.tensor_scalar(out=x2[i][:], in0=x2[i][:], scalar1=0.044715, scalar2=1.0, op0=ALU.mult, op1=ALU.add))
            sc.op("dve", [kx, ky], [kx], lambda h: h.tensor_tensor(out=x2[i][:], in0=x2[i][:], in1=yy[i][:], op=ALU.mult))
            sc.op("act", [kx], [kx], lambda h: h.activation(out=x2[i][:], in_=x2[i][:], func=AF.Sigmoid, scale=1.5957691216057308))
            sc.op("dve", [kx, ky], [kz], lambda h: h.tensor_tensor(out=za[i][:], in0=x2[i][:], in1=yy[i][:], op=ALU.mult))
            sc.op("dve", [kz], [kzb], lambda h: h.tensor_copy(out=zb[i][:], in_=za[i][:]))
            for n in range(2):
                pst, pk = C.ps[(2 * t + n) % 4], f"ps{(2 * t + n) % 4}"
                def mm(h, n=n, pst=pst):
                    for k in range(2):
                        ins = h.matmul(pst[:], wg[:, k, n * 128:(n + 1) * 128], zb[i][:, k, :], start=(k == 0), stop=(k == 1))
                    return ins
                sc.op("pe", [kzb, "gwg"], [pk], mm)
                sk = f"gsg{n}"
                sc.op("act", [pk, "vecs"], [sk], lambda h, n=n, pst=pst: h.activation(out=sg[n][:], in_=pst[:], func=AF.Sigmoid, bias=V[:, 38 + n:39 + n]))
                sc.op("dve", [sk, kz], [ko], lambda h, n=n: h.tensor_tensor(out=ob[i][:, n, :], in0=za[i][:, n, :], in1=sg[n][:], op=ALU.mult))
            sc.dma("sp", S["Y"][768:1024, cs].rearrange("(k p) n -> p k n", p=128), ob[i][:], [ko], ["Y"])
    sc.barrier()


def ffn_up(C, I, S, L):
    sc, nc = C.sc, C.nc
    TG = 2048
    with ExitStack() as ph:
        C.ph = ph
        cp = psb(C, "fcp", [128, 44, 4])
        sc.dma("sp", cp[:], I["convp"][L], [], ["fcp"])
        halo = psb(C, "fhalo", [128, 44, 2])
        sc.op("dve", [], ["fhalo"], lambda h: h.memset(halo[:], 0.0))
        U = [psb(C, f"fU{i}", [128, TG + 2]) for i in range(3)]
        cv = [psb(C, f"fcv{i}", [128, TG]) for i in range(3)]
        tm = [psb(C, f"ftm{i}", [128, TG]) for i in range(2)]
        gs = [psb(C, f"fgs{i}", [128, TG]) for i in range(2)]
        ab = [psb(C, f"fab{i}", [128, TG], BF16) for i in range(2)]
        pend = []
        def evac(pst, pk, j, tok0, cnt):
            u_, uk = U[j % 3], f"fU{j % 3}"
            t = (tok0 % TG) // TT
            if t == 0:
                sc.op("dve", ["fhalo"], [uk], lambda h: h.tensor_copy(out=u_[:, 0:2], in_=halo[:, j, :]))
            sc.op("act", [pk], [uk], lambda h: h.activation(out=u_[:, 2 + t * TT:2 + (t + 1) * TT], in_=pst[:], func=AF.Identity))
        def after(j, g):
            u_, uk = U[j % 3], f"fU{j % 3}"
            c_, ck = cv[j % 3], f"fcv{j % 3}"
            t_, tk = tm[j % 2], f"ftm{j % 2}"
            gi = (j // 2) % 2
            while pend:
                pend.pop(0)()
            sc.op("dve", [uk], ["fhalo"], lambda h: h.tensor_copy(out=halo[:, j, :], in_=u_[:, TG:TG + 2]))
            sc.op("pool", [uk, "fcp"], [ck], lambda h: h.tensor_scalar(out=c_[:], in0=u_[:, 0:TG], scalar1=cp[:, j, 0:1], scalar2=cp[:, j, 3:4], op0=ALU.mult, op1=ALU.add))
            sc.op("dve", [uk, "fcp", ck], [ck], lambda h: h.scalar_tensor_tensor(out=c_[:], in0=u_[:, 1:TG + 1], scalar=cp[:, j, 1:2], in1=c_[:], op0=ALU.mult, op1=ALU.add))
            sc.op("dve", [uk, "fcp", ck], [ck], lambda h: h.scalar_tensor_tensor(out=c_[:], in0=u_[:, 2:TG + 2], scalar=cp[:, j, 2:3], in1=c_[:], op0=ALU.mult, op1=ALU.add))
            if j % 2 == 0:
                pend.append(lambda: sc.op("act", [ck], [f"fgs{gi}"], lambda h: h.activation(out=gs[gi][:], in_=c_[:], func=AF.Silu)))
            else:
                a_, ak = ab[gi], f"fab{gi}"
                sc.op("dve", [ck, f"fgs{gi}"], [ak], lambda h: h.tensor_tensor(out=a_[:], in0=gs[gi][:], in1=c_[:], op=ALU.mult))
                r0 = (j // 2) * 128
                sc.dma("sp", S["ACT"][r0:r0 + 128, g * TG:(g + 1) * TG], a_[:], [ak], ["ACT"])
        linear_fm(C, S["HN"], 0, 8, I["w_up"][L], 44, TG, evac, after=after, cast_eng="act")
    sc.barrier()


def ple(C, I, S, L):
    sc, nc = C.sc, C.nc
    with ExitStack() as ph:
        C.ph = ph
        pb16 = psb(C, "plp", [128, 2, 2048], BF16)
        ev = store_evac(C, lambda j: (S["PB"], 0, BF16)) if False else None
    with ExitStack() as ph:
        C.ph = ph
        tb = [psb(C, f"plt{i}", [128, 2, TT], BF16) for i in range(2)]
        for t in range(NTT):
            i = t % 2
            cs = slice(t * TT, (t + 1) * TT)
            sc.dma("pool", tb[i][:], I["pT"][L][:, cs].rearrange("(k p) n -> p k n", p=128), [], [f"plt{i}"])
            sc.dma("sp", S["PB"][:, cs].rearrange("(k p) n -> p k n", p=128), tb[i][:], [f"plt{i}"], ["PB"])
    sc.barrier()
    with ExitStack() as ph:
        C.ph = ph
        TG = 2048
        a = psb(C, "pla", [128, 8, TG], BF16)
        pa = psb(C, "plpa", [128, 2, TG], BF16)
        wg = [psb(C, f"plwg{i}", [128, 8, 128], BF16) for i in range(2)]
        wp = [psb(C, f"plwp{i}", [128, 2, 128], BF16) for i in range(2)]
        sg = [psb(C, f"plsg{i}", [128, TT]) for i in range(2)]
        hb = [psb(C, f"plh{i}", [128, TT]) for i in range(3)]
        Wg = I["w_pg"][L].rearrange("(k p) n -> p k n", p=128)
        Wp = I["w_ple"][L].rearrange("(k p) n -> p k n", p=128)
        cnt = 0
        for g in range(NT // TG):
            gs_ = slice(g * TG, (g + 1) * TG)
            sc.dma("sp", a[:], S["HN"][:, gs_].rearrange("(k p) n -> p k n", p=128), ["HN"], ["pla"])
            sc.dma("sp", pa[:], S["PB"][:, gs_].rearrange("(k p) n -> p k n", p=128), ["PB"], ["plpa"])
            for j in range(8):
                wi = j % 2
                sc.dma("pool", wg[wi][:], Wg[:, :, j * 128:(j + 1) * 128], [], [f"plwg{wi}"])
                sc.dma("pool", wp[wi][:], Wp[:, :, j * 128:(j + 1) * 128], [], [f"plwp{wi}"])
                for t in range(TG // TT):
                    p1, p2 = C.ps[(2 * cnt) % 4], C.ps[(2 * cnt + 1) % 4]
                    k1, k2 = f"ps{(2 * cnt) % 4}", f"ps{(2 * cnt + 1) % 4}"
                    hi_ = cnt % 3
                    si = cnt % 2
                    cnt += 1
                    ts_ = slice(t * TT, (t + 1) * TT)
                    def mm(h, p1=p1, p2=p2, wi=wi, ts_=ts_):
                        for k in range(8):
                            h.matmul(p1[:], wg[wi][:, k, :], a[:, k, ts_], start=(k == 0), stop=(k == 7))
                        for k in range(2):
                            ins = h.matmul(p2[:], wp[wi][:, k, :], pa[:, k, ts_], start=(k == 0), stop=(k == 1))
                        return ins
                    sc.op("pe", ["pla", "plpa", f"plwg{wi}", f"plwp{wi}"], [k1, k2], mm)
                    sc.op("act", [k1], [f"plsg{si}"], lambda h, p1=p1, si=si: h.activation(out=sg[si][:], in_=p1[:], func=AF.Sigmoid))
                    sc.op("dve", [k2, f"plsg{si}"], [f"plsg{si}"], lambda h, p2=p2, si=si: h.tensor_tensor(out=sg[si][:], in0=p2[:], in1=sg[si][:], op=ALU.mult))
                    hv = S["H"][j * 128:(j + 1) * 128, g * TG + t * TT:g * TG + (t + 1) * TT]
                    sc.dma("pool", hb[hi_][:], hv, ["H"], [f"plh{hi_}"])
                    sc.op("dve", [f"plsg{si}", f"plh{hi_}"], [f"plh{hi_}"], lambda h, si=si, hi_=hi_: h.tensor_tensor(out=hb[hi_][:], in0=hb[hi_][:], in1=sg[si][:], op=ALU.add))
                    sc.dma("sp", hv, hb[hi_][:], [f"plh{hi_}"], ["H"])
    sc.barrier()


_NC = None


def _prep(inp):
    f = lambda a: np.ascontiguousarray(np.asarray(a), dtype=np.float32)
    g = {k: np.asarray(v) for k, v in inp.items()}
    Lr = range(DEPTH)
    com = {}
    com["ident"] = np.eye(128, dtype=np.float32)
    am = np.zeros((128, 4, 512), np.float32)
    s_ = np.arange(128)[:, None]
    t_ = np.arange(512)[None, :]
    for m in range(4):
        am[:, m, :] = (128 * m + s_ <= t_)
    com["amask"] = am
    com["amadd"] = ((am - 1.0) * 30000.0).astype(np.float32)
    cv = np.zeros((128, 16), np.float32)
    cv[:, 0] = EPS
    cv[0:16, 1] = -1.0; cv[16:32, 1] = 1.0
    cv[0:64, 2] = 1.0
    cv[64:128, 3] = -1.0
    cv[64:128, 4] = 1.0
    cv[0:64, 5] = 1.0; cv[64:128, 5] = -1.0
    for k in range(8):
        cv[16 * k:16 * k + 16, 8 + k] = 1.0
    com["cvec"] = cv
    half = 16
    invf = (np.float32(10000.0) ** (-np.arange(half, dtype=np.float32) / np.float32(half))).astype(np.float32)
    com["invf"] = np.concatenate([invf, invf])[:, None].astype(np.float32)
    w_in = f(g["w_in"])
    sw = np.concatenate([np.arange(16, 32), np.arange(0, 16)])
    wfm = np.zeros((DEPTH, D, NIN_FM * 128), np.float32)
    wtm = np.zeros((DEPTH, D, 512), np.float32)
    wfm[:, :, 0:256] = w_in[:, :, 0:256]
    wfm[:, :, 256:384] = w_in[:, :, 256:384]
    wfm[:, :, 384 + 64:384 + 96] = w_in[:, :, 384:416]
    wfm[:, :, 512 + 64:512 + 96] = w_in[:, :, 384:416][:, :, sw]
    wfm[:, :, 640:896] = w_in[:, :, 416:672]
    wfm[:, :, 896:1152] = w_in[:, :, 672:928]
    wfm[:, :, 1152:1408] = w_in[:, :, 1188:1444]
    wfm[:, :, 1408:1664] = w_in[:, :, 1956:2212]
    wfm[:, :, 1664:1920] = w_in[:, :, 2212:2468]
    wfm[:, :, 1920:2176] = w_in[:, :, 1444:1700]
    wfm[:, :, 2176:2180] = w_in[:, :, 1184:1188]
    wtm[:, :, 0:256] = w_in[:, :, 928:1184]
    wtm[:, :, 256:512] = w_in[:, :, 1700:1956]
    com["w_in_fm"], com["w_in_tm"] = wfm, wtm
    wuq = f(g["mla_w_uq"]).reshape(DEPTH, 256, 4, 96)
    wq = np.zeros((DEPTH, 256, 4, 2, 128), np.float32)
    wq[:, :, :, 0, 0:96] = wuq
    wq[:, :, :, 1, 64:96] = wuq[:, :, :, 64:96][:, :, :, sw]
    com["w_uq"] = wq.reshape(DEPTH, 256, 1024)
    wukv = f(g["mla_w_ukv"]).reshape(DEPTH, 128, 4, 128)
    wk = np.zeros((DEPTH, 128, 4, 128), np.float32)
    wk[:, :, :, 0:64] = wukv[:, :, :, 0:64]
    com["w_uk"] = wk.reshape(DEPTH, 128, 512)
    com["w_uv"] = np.ascontiguousarray(wukv[:, :, :, 64:128]).reshape(DEPTH, 128, 256)
    com["w_out"] = f(g["w_out"])
    wup = f(g["w_up"]).reshape(DEPTH, D, 2, 22, 128).transpose(0, 1, 3, 2, 4)
    com["w_up"] = np.ascontiguousarray(wup).reshape(DEPTH, D, 2 * DFF)
    com["w_down"] = f(g["w_down"])
    com["w_pg"] = f(g["w_ple_gate"])
    com["w_ple"] = f(g["w_ple"])
    com["w_glu"] = f(g["s5_w_glu"])
    vec = np.zeros((DEPTH, 128, 64), np.float32)
    pk = lambda a, n: f(a).reshape(DEPTH, n, 128).transpose(0, 2, 1)
    vec[:, :, 0:8] = pk(g["attn_norm_g"], 8)
    vec[:, :, 8:16] = pk(g["ffn_norm_g"], 8)
    vec[:, :, 16:24] = pk(g["ple_norm_g"], 8)
    vec[:, :, 24:32] = pk(g["group_norm_g"], 8)
    vec[:, :, 32:34] = pk(g["mla_q_norm_g"], 2)
    vec[:, :, 34:35] = pk(g["mla_kv_norm_g"], 1)
    vec[:, 0:4, 35] = f(g["fox_b_f"])
    vec[:, :, 36:38] = pk(g["s5_d"], 2)
    vec[:, :, 38:40] = pk(g["s5_b_glu"], 2)
    com["vecs"] = vec
    cw = f(g["conv_w"]).reshape(DEPTH, 3, 2, 22, 128)
    cbi = f(g["conv_b"]).reshape(DEPTH, 1, 2, 22, 128)
    cpk = np.concatenate([cw, cbi], axis=1)
    com["convp"] = np.ascontiguousarray(cpk.transpose(0, 4, 3, 2, 1)).reshape(DEPTH, 128, 44, 4)
    lre, lim, lst = f(g["s5_lam_re"]), f(g["s5_lam_im"]), f(g["s5_log_step"])
    s5a = np.zeros((DEPTH, 128, 3, 16), np.float32)
    for hf in range(2):
        s5a[:, 64 * hf:64 * hf + 64, 0, :] = lre.transpose(0, 2, 1)
        s5a[:, 64 * hf:64 * hf + 64, 1, :] = lim.transpose(0, 2, 1)
        s5a[:, 64 * hf:64 * hf + 64, 2, :] = lst[:, None, :]
    com["s5a"] = s5a
    s5b = np.zeros((DEPTH, 128, 2, 5, 64), np.float32)
    bre, bim = f(g["s5_b_re"]), f(g["s5_b_im"])
    for gg in range(16):
        rows = slice(16 * (gg % 8), 16 * (gg % 8) + 16)
        tl = gg // 8
        s5b[:, rows, tl, 0, :] = lre[:, gg, None, :]
        s5b[:, rows, tl, 1, :] = lim[:, gg, None, :]
        s5b[:, rows, tl, 2, :] = lst[:, gg, None, None]
        s5b[:, rows, tl, 3, :] = bre[:, gg].transpose(0, 2, 1)
        s5b[:, rows, tl, 4, :] = bim[:, gg].transpose(0, 2, 1)
    com["s5b"] = s5b
    cre, cim = f(g["s5_c_re"]), f(g["s5_c_im"])
    com["s5c"] = np.ascontiguousarray(np.concatenate([cre.transpose(0, 3, 1, 2), cim.transpose(0, 3, 1, 2)], axis=1))
    com["lbp"] = np.ascontiguousarray(f(g["hgrn_lb_param"]).reshape(DEPTH, 2, 128).transpose(2, 1, 0))
    com["fing"] = np.ascontiguousarray(f(g["final_norm_g"]).reshape(8, 128).T)
    x, p, pos = f(g["x"]), f(g["p"]), np.asarray(g["positions"]).astype(np.int32)
    maps = []
    for c in range(8):
        b = c % 2
        m = dict(com)
        m["xT"] = np.ascontiguousarray(x[b].T)
        m["pT"] = np.ascontiguousarray(p[:, b].transpose(0, 2, 1))
        m["pos"] = np.ascontiguousarray(pos[b][None, :])
        maps.append(m)
    return maps


def kernel(**inputs):
    global _NC
    maps = _prep(inputs)
    if _NC is None:
        _NC = build()
    res = run_bass_kernel_spmd(_NC, maps, core_ids=list(range(8)))
    global _LAST
    _LAST = res.results
    outs = [np.asarray(res.results[b]["out"]).T for b in range(2)]
    return np.ascontiguousarray(np.stack(outs, 0)).astype(np.float32)
```

```python
import math
import os
from contextlib import ExitStack
import numpy as np
import concourse.bass as bass
import concourse.mybir as mybir
from concourse.bass_utils import run_bass_kernel_spmd

F32 = mybir.dt.float32
BF16 = mybir.dt.bfloat16
I32 = mybir.dt.int32
AF = mybir.ActivationFunctionType
ALU = mybir.AluOpType

NT = 8192
D = 1024
DEPTH = 2
TT = 512
NTT = NT // TT
EPS = 1e-6
DFF = 2816
NIN_FM = 18
TWO_PI = 6.283185307179586
DBG = []


class Sched:
    EPOCH = 16000
    DEPOCH = 30000
    K = 4

    def __init__(self, nc, st):
        self.nc, self.st = nc, st
        self.nsem = 0
        self.eng = {}
        for name, h in (("pe", nc.tensor), ("act", nc.scalar), ("dve", nc.vector), ("pool", nc.gpsimd), ("sp", nc.sync)):
            self.eng[name] = dict(h=h, sem=self._sem(), cnt=0, seen={}, last=None)
        self.dq = {q: dict(slots=[dict(sem=self._sem(), val=0) for _ in range(self.K)], n=0) for q in ("sp", "act", "pool")}
        self.last_w, self.reads = {}, {}

    def _sem(self):
        self.nsem += 1
        s = self.st.enter_context(self.nc.semaphore(f"sm{self.nsem}"))
        return (self.nsem, s)

    def _wait(self, en, ev):
        (sid, sem), val, src = ev
        e = self.eng[en]
        if e["seen"].get(sid, 0) >= val:
            return
        e["h"].wait_ge(sem, val)
        e["seen"][sid] = val

    def _deps(self, en, reads, writes):
        for k in reads:
            w = self.last_w.get(k)
            if w is not None:
                self._wait(en, w)
        for k in writes:
            w = self.last_w.get(k)
            if w is not None and not (w[2] == en == "pe"):
                self._wait(en, w)
            for ev in self.reads.get(k, {}).values():
                if not (ev[2] == en == "pe"):
                    self._wait(en, ev)

    def _record(self, ev, reads, writes):
        for k in writes:
            self.last_w[k] = ev
            self.reads[k] = {}
        for k in reads:
            self.reads.setdefault(k, {})[ev[0][0]] = ev

    def op(self, en, reads, writes, fn):
        e = self.eng[en]
        self._deps(en, reads, writes)
        ins = fn(e["h"])
        if e["cnt"] >= self.EPOCH:
            e["sem"], e["cnt"] = self._sem(), 0
        e["cnt"] += 1
        ins.then_inc(e["sem"][1], 1)
        ev = (e["sem"], e["cnt"], en)
        e["last"] = ev
        self._record(ev, reads, writes)

    def dma(self, q, out, in_, reads, writes):
        d = self.dq[q]
        self._deps(q, reads, writes)
        sl = d["slots"][d["n"] % self.K]
        d["n"] += 1
        if sl["val"] > 0:
            self._wait(q, (sl["sem"], sl["val"], "dma"))
        if sl["val"] >= self.DEPOCH:
            sl["sem"], sl["val"] = self._sem(), 0
        self.eng[q]["h"].dma_start(out=out, in_=in_).then_inc(sl["sem"][1], 16)
        sl["val"] += 16
        ev = (sl["sem"], sl["val"], "dma")
        self._record(ev, reads, writes)

    def barrier(self):
        evs = [e["last"] for e in self.eng.values() if e["last"] is not None]
        for d in self.dq.values():
            for sl in d["slots"]:
                if sl["val"] > 0:
                    evs.append((sl["sem"], sl["val"], "dma"))
        for en in self.eng:
            for ev in evs:
                self._wait(en, ev)
        self.last_w, self.reads = {}, {}


class Ctx:
    pass


def fm(ap, r0, nrows):
    if nrows % 128 == 0 and nrows > 128:
        return ap[r0:r0 + nrows, :].rearrange("(k p) n -> p k n", p=128)
    return ap[r0:r0 + nrows, :]


def build():
    nc = bass.Bass("TRN2", target_bir_lowering=False)
    C = Ctx()
    C.nc = nc
    di = lambda name, shape, dt=F32: nc.dram_tensor(name, list(shape), dt, kind="ExternalInput").ap()
    dbgset = set(filter(None, os.environ.get("KDBG", "").split(",")))
    C.limit = int(os.environ.get("KSTOP", "100000"))
    C.phase = 0
    ds_ = lambda name, shape, dt=BF16: nc.dram_tensor(name, list(shape), dt, kind=("ExternalOutput" if name in dbgset else "Internal")).ap()
    I = {}
    I["xT"] = di("xT", [D, NT])
    I["pT"] = di("pT", [DEPTH, 256, NT])
    I["pos"] = di("pos", [1, NT], I32)
    I["ident"] = di("ident", [128, 128])
    I["amask"] = di("amask", [128, 4, 512])
    I["cvec"] = di("cvec", [128, 16])
    I["amadd"] = di("amadd", [128, 4, 512])
    I["invf"] = di("invf", [32, 1])
    I["w_in_fm"] = di("w_in_fm", [DEPTH, D, NIN_FM * 128])
    I["w_in_tm"] = di("w_in_tm", [DEPTH, D, 512])
    I["w_uq"] = di("w_uq", [DEPTH, 256, 1024])
    I["w_uk"] = di("w_uk", [DEPTH, 128, 512])
    I["w_uv"] = di("w_uv", [DEPTH, 128, 256])
    I["w_out"] = di("w_out", [DEPTH, D, D])
    I["w_up"] = di("w_up", [DEPTH, D, 2 * DFF])
    I["w_down"] = di("w_down", [DEPTH, DFF, D])
    I["w_pg"] = di("w_pg", [DEPTH, D, D])
    I["w_ple"] = di("w_ple", [DEPTH, 256, D])
    I["w_glu"] = di("w_glu", [DEPTH, 256, 256])
    I["vecs"] = di("vecs", [DEPTH, 128, 64])
    I["convp"] = di("convp", [DEPTH, 128, 44, 4])
    I["s5a"] = di("s5a", [DEPTH, 128, 3, 16])
    I["s5b"] = di("s5b", [DEPTH, 128, 2, 5, 64])
    I["s5c"] = di("s5c", [DEPTH, 128, 16, 16])
    I["lbp"] = di("lbp", [128, 2, 2])
    I["fing"] = di("fing", [128, 8])
    out = nc.dram_tensor("out", [D, NT], F32, kind="ExternalOutput").ap()
    S = {}
    S["H"] = ds_("H", [D, NT], F32)
    S["HN"] = ds_("HN", [D, NT])
    S["PF"] = ds_("PF", [NIN_FM * 128, NT])
    S["PF32"] = ds_("PF32", [384, NT], F32)
    S["PTM"] = ds_("PTM", [NT, 512])
    S["CQN"] = ds_("CQN", [256, NT])
    S["KVN"] = ds_("KVN", [128, NT])
    S["QA"] = ds_("QA", [4, 128, NT])
    S["KA"] = ds_("KA", [4, 128, NT])
    S["VM"] = ds_("VM", [NT, 256])
    S["ROPE"] = ds_("ROPE", [2, 32, NT], F32)
    S["FAUG"] = ds_("FAUG", [4, 6, NT])
    S["FBIAS"] = ds_("FBIAS", [128, 64, 4], F32)
    S["Y"] = ds_("Y", [D, NT])
    S["YS5"] = ds_("YS5", [256, NT], F32)
    S["YN"] = ds_("YN", [D, NT])
    S["ACT"] = ds_("ACTB", [DFF, NT])
    S["PB"] = ds_("PB", [256, NT])

    with ExitStack() as st:
        sc = Sched(nc, st)
        C.sc = sc
        sb = lambda name, shape, dt=F32: st.enter_context(nc.sbuf_tensor("c_" + name, list(shape), dt))
        C.psall = st.enter_context(nc.psum_tensor("psall", [128, 4096], F32))
        C.ps = [C.psall[:, i * 512:(i + 1) * 512] for i in range(8)]
        C.ident = sb("ident", [128, 128])
        C.identb = sb("identb", [128, 128], BF16)
        C.onesb = sb("onesb", [128, 128], BF16)
        C.onesf = sb("onesf", [128, 2048])
        C.amask = sb("amaskb", [128, 4, 512], BF16)
        C.cvec = sb("cvec", [128, 16])
        C.amadd = sb("amadd", [128, 4, 512])
        C.vecs = sb("vecs", [128, 64])
        sc.dma("sp", C.ident[:], I["ident"], [], ["ident"])
        sc.dma("pool", C.identb[:], I["ident"], [], ["identb"])
        sc.dma("pool", C.amask[:], I["amask"], [], ["amask"])
        sc.dma("sp", C.cvec[:], I["cvec"], [], ["cvec"])
        sc.dma("sp", C.amadd[:], I["amadd"], [], ["amadd"])
        sc.op("dve", [], ["onesb"], lambda h: h.memset(C.onesb[:], 1.0))
        sc.op("dve", [], ["onesf"], lambda h: h.memset(C.onesf[:], 1.0))
        for k in range(16):
            sc.dma("sp", S["H"][k * 64:(k + 1) * 64, :], I["xT"][k * 64:(k + 1) * 64, :], [], ["H"])
        rope_tables(C, I, S)
        sc.barrier()
        for L in range(DEPTH):
            sc.dma("sp", C.vecs[:], I["vecs"][L], [], ["vecs"])
            layer(C, I, S, L)
        with ExitStack() as ph:
            C.ph = ph
            g = psb(C, "fg", [128, 8])
            sc.dma("sp", g[:], I["fing"], [], ["fg"])
            rmsnorm_fm(C, S["H"], 0, 8, g, ["fg"], out, 0, F32, 1024)
        sc.barrier()
    return nc


_UID = [0]


def psb(C, name, shape, dt=F32):
    _UID[0] += 1
    return C.ph.enter_context(C.nc.sbuf_tensor(f"{name}_u{_UID[0]}", list(shape), dt))


def rmsnorm_fm(C, src, r0, KT, g, gkeys, dst, d0, ddt, Dn, gcol0=0, post=None, tag="rn"):
    sc, nc = C.sc, C.nc
    sdt = src.tensor.dtype if hasattr(src, "tensor") else F32
    xs = [psb(C, f"{tag}x{i}", [128, KT, TT], sdt) for i in range(2)]
    sq = [psb(C, f"{tag}q{i}", [128, KT, TT], BF16) for i in range(2)]
    rs = [psb(C, f"{tag}r{i}", [128, TT]) for i in range(2)]
    os_ = [psb(C, f"{tag}o{i}", [128, KT, TT], ddt) for i in range(2)]
    for t in range(NTT):
        i = t % 2
        x, q, r, o = xs[i], sq[i], rs[i], os_[i]
        kx, kq, kr, ko = f"{tag}x{i}", f"{tag}q{i}", f"{tag}r{i}", f"{tag}o{i}"
        pk = f"ps{6 + i}"
        pst = C.ps[6 + i]
        cs = slice(t * TT, (t + 1) * TT)
        sv = src[r0:r0 + KT * 128, cs]
        sv = sv.rearrange("(k p) n -> p k n", p=128)
        sc.dma("pool", x[:], sv, ["H", "Y", "PF"], [kx])
        sc.op("act", [kx], [kq], lambda h: h.activation(out=q[:], in_=x[:], func=AF.Square))
        def mm(h):
            for k in range(KT):
                ins = h.matmul(pst[:], C.onesb[:], q[:, k, :], start=(k == 0), stop=(k == KT - 1))
            return ins
        sc.op("pe", [kq, "onesb"], [pk], mm)
        sc.op("act", [pk], [kr], lambda h: h.activation(out=r[:], in_=pst[:], func=AF.Sqrt, scale=1.0 / Dn, bias=C.cvec[:, 0:1]))
        sc.op("dve", [kr], [kr], lambda h: h.reciprocal(out=r[:], in_=r[:]))
        for k in range(KT):
            sc.op("dve", [kx, kr] + gkeys, [ko], lambda h, k=k: h.scalar_tensor_tensor(
                out=o[:, k, :], in0=x[:, k, :], scalar=g[:, gcol0 + k:gcol0 + k + 1], in1=r[:], op0=ALU.mult, op1=ALU.mult))
        if post is not None:
            post(o, ko, t)
        dv = dst[d0:d0 + KT * 128, cs].rearrange("(k p) n -> p k n", p=128)
        sc.dma("sp", dv, o[:], [ko], ["HN", "YN", "CQN", "KVN", "out"])


def linear_fm(C, src, r0, KT, W, n_tiles, TG, evac, tag="lf", after=None, WB=None, cast_eng="pool"):
    sc, nc = C.sc, C.nc
    WB = WB or (4 if KT <= 8 else 1)
    a = psb(C, f"{tag}a", [128, KT, TG], BF16)
    ws = [psb(C, f"{tag}w{i}", [128, KT, 128], BF16) for i in range(3)]
    stg = [psb(C, f"{tag}s{i}", [128, KT, WB * 128]) for i in range(2)]
    Wv = W.rearrange("(k p) n -> p k n", p=128)
    blocks = [(g, jb) for g in range(NT // TG) for jb in range(0, n_tiles, WB)]
    def load_block(bi):
        g, jb = blocks[bi]
        nb = min(WB, n_tiles - jb)
        sc.dma("act", stg[bi % 2][:, :, 0:nb * 128], Wv[:, :, jb * 128:(jb + nb) * 128], [], [f"{tag}s{bi % 2}"])
    load_block(0)
    cnt = 0
    for bi, (g, jb) in enumerate(blocks):
        if jb == 0:
            sv = src[r0:r0 + KT * 128, g * TG:(g + 1) * TG].rearrange("(k p) n -> p k n", p=128)
            sc.dma("pool", a[:], sv, ["HN", "YN", "ACT", "CQN", "KVN"], [f"{tag}a"])
        if bi + 1 < len(blocks):
            load_block(bi + 1)
        nb = min(WB, n_tiles - jb)
        for jj in range(nb):
            j = jb + jj
            w = ws[j % 3]
            wk = f"{tag}w{j % 3}"
            if cast_eng == "act":
                sc.op("act", [f"{tag}s{bi % 2}"], [wk], lambda h, w=w, jj=jj, bi=bi: h.activation(out=w[:], in_=stg[bi % 2][:, :, jj * 128:(jj + 1) * 128], func=AF.Identity))
            else:
                sc.op("pool", [f"{tag}s{bi % 2}"], [wk], lambda h, w=w, jj=jj, bi=bi: h.tensor_copy(out=w[:], in_=stg[bi % 2][:, :, jj * 128:(jj + 1) * 128]))
            for t in range(TG // TT):
                pi = cnt % 4
                cnt += 1
                pst, pk = C.ps[pi], f"ps{pi}"
                def mm(h, t=t, w=w, pst=pst):
                    for k in range(KT):
                        ins = h.matmul(pst[:], w[:, k, :], a[:, k, t * TT:(t + 1) * TT], start=(k == 0), stop=(k == KT - 1))
                    return ins
                sc.op("pe", [wk, f"{tag}a"], [pk], mm)
                evac(pst, pk, j, g * TG + t * TT, cnt)
            if after is not None:
                after(j, g)


def store_evac(C, dst_of, tag="se"):
    sc = C.sc
    stg = {}
    def evac(pst, pk, j, tok0, cnt):
        ap, row0, dt = dst_of(j)
        i = cnt % 3
        key = f"{tag}{dt}{i}"
        if key not in stg:
            stg[key] = psb(C, key, [128, TT], dt)
        s = stg[key]
        if cnt % 2 == 0:
            sc.op("act", [pk], [key], lambda h: h.activation(out=s[:], in_=pst[:], func=AF.Identity))
        else:
            sc.op("dve", [pk], [key], lambda h: h.tensor_copy(out=s[:], in_=pst[:]))
        sc.dma("sp", ap[row0:row0 + 128, tok0:tok0 + TT], s[:], [key], ["PF", "PB"])
    return evac


def resid_evac(C, S, tag="re", mul=None):
    sc = C.sc
    hb = [psb(C, f"{tag}h{i}", [128, TT]) for i in range(3)]
    def evac(pst, pk, j, tok0, cnt):
        i = cnt % 3
        h_, hk = hb[i], f"{tag}h{i}"
        hv = S["H"][j * 128:(j + 1) * 128, tok0:tok0 + TT]
        sc.dma("pool", h_[:], hv, ["H"], [hk])
        sc.op("dve", [pk, hk], [hk], lambda h: h.tensor_tensor(out=h_[:], in0=pst[:], in1=h_[:], op=ALU.add))
        sc.dma("sp", hv, h_[:], [hk], ["H"])
    return evac


def rope_tables(C, I, S):
    sc, nc = C.sc, C.nc
    with ExitStack() as ph:
        C.ph = ph
        pi = psb(C, "rp_i", [32, 2048], I32)
        pf = psb(C, "rp_f", [32, 2048])
        an = psb(C, "rp_a", [32, 2048])
        kk = psb(C, "rp_k", [32, 2048], I32)
        kf = psb(C, "rp_kf", [32, 2048])
        o = psb(C, "rp_o", [32, 2048])
        invf = psb(C, "rp_inv", [32, 1])
        sc.dma("sp", invf[:], I["invf"], [], ["invf"])
        for g in range(4):
            cs = slice(g * 2048, (g + 1) * 2048)
            src = bass.AP(I["pos"].tensor, g * 2048, [[0, 32], [1, 2048]])
            sc.dma("sp", pi[:], src, [], ["rp_i"])
            sc.op("dve", ["rp_i"], ["rp_f"], lambda h: h.tensor_copy(out=pf[:], in_=pi[:]))
            sc.op("dve", ["rp_f", "invf"], ["rp_f"], lambda h: h.tensor_scalar(out=pf[:], in0=pf[:], scalar1=invf[:, 0:1], scalar2=None, op0=ALU.mult))
            for which, shift in ((0, math.pi / 2), (1, 0.0)):
                sc.op("dve", ["rp_f"], ["rp_a"], lambda h: h.tensor_scalar(out=an[:], in0=pf[:], scalar1=shift, scalar2=None, op0=ALU.add))
                sc.op("dve", ["rp_a"], ["rp_k"], lambda h: h.tensor_scalar(out=kk[:], in0=an[:], scalar1=1.0 / TWO_PI, scalar2=None, op0=ALU.mult))
                sc.op("dve", ["rp_k"], ["rp_kf"], lambda h: h.tensor_copy(out=kf[:], in_=kk[:]))
                sc.op("dve", ["rp_kf", "rp_a"], ["rp_a"], lambda h: h.scalar_tensor_tensor(out=an[:], in0=kf[:], scalar=-TWO_PI, in1=an[:], op0=ALU.mult, op1=ALU.add))
                sc.op("dve", ["rp_a"], ["rp_a"], lambda h: h.tensor_scalar(out=an[:], in0=an[:], scalar1=3.1415925, scalar2=-3.1415925, op0=ALU.min, op1=ALU.max))
                sc.op("act", ["rp_a"], ["rp_o"], lambda h: h.activation(out=o[:], in_=an[:], func=AF.Sin))
                if which == 1:
                    sc.op("dve", ["rp_o", "cvec"], ["rp_o"], lambda h: h.tensor_scalar(out=o[:], in0=o[:], scalar1=C.cvec[0:32, 1:2], scalar2=None, op0=ALU.mult))
                sc.dma("sp", S["ROPE"][which, :, cs], o[:], ["rp_o"], ["ROPE"])


def attention_multi(C, S, heads, use_bias):
    sc, nc = C.sc, C.nc
    with ExitStack() as ph:
        C.ph = ph
        QTs = [psb(C, f"aQ{i}", [128, NT], BF16) for i in range(2)]
        KTs = [psb(C, f"aK{i}", [128, NT], BF16) for i in range(2)]
        VAs = [psb(C, f"aV{i}", [128, 64, 128], BF16) for i in range(2)]
        for i in range(2):
            sc.op("pool", [], [f"aQ{i}"], lambda h, i=i: h.memset(QTs[i][:], 0.0))
            sc.op("pool", [], [f"aK{i}"], lambda h, i=i: h.memset(KTs[i][:], 0.0))
            sc.op("pool", [], [f"aV{i}"], lambda h, i=i: h.memset(VAs[i][:], 0.0))
        P = [psb(C, f"aP{i}", [128, 1024], BF16) for i in range(3)]
        osb = [psb(C, f"aO{i}", [65, TT]) for i in range(2)]
        rec = [psb(C, f"aR{i}", [64, TT]) for i in range(2)]
        yb = [psb(C, f"aY{i}", [64, TT], BF16) for i in range(2)]
        dtmp = [psb(C, f"aD{i}", [128, TT]) for i in range(2)]
        bias = None
        if use_bias:
            bias = psb(C, "aB", [128, 64, 4])
            sc.dma("pool", bias[:], S["FBIAS"], ["FBIAS"], ["aB"])

        def loads(hi):
            H = heads[hi]
            b_ = hi % 2
            QT, KT_, VA = QTs[b_], KTs[b_], VAs[b_]
            if H["kones"] is not None:
                k0, kn = H["kones"]
                sc.op("dve", [], [f"aK{b_}"], lambda h: h.memset(KT_[k0:k0 + kn, :], 1.0))
                sc.op("dve", [], [f"aQ{b_}"], lambda h: h.memset(QT[k0:k0 + kn, :], 1.0))
            for ap, p0, n in H["q"]:
                sc.dma("pool", QT[p0:p0 + n, :], ap, ["PF", "QA", "FAUG"], [f"aQ{b_}"])
            for ap, p0, n in H["k"]:
                sc.dma("pool", KT_[p0:p0 + n, :], ap, ["PF", "KA", "FAUG"], [f"aK{b_}"])
            sc.op("dve", [], [f"aV{b_}"], lambda h: h.memset(VA[:, :, 64:65], 1.0))
            for b4 in range(4):
                vv = H["v"][b4 * 2048:(b4 + 1) * 2048, H["vcol"]:H["vcol"] + 64].rearrange("(b p) d -> p b d", p=128)
                sc.dma("pool", VA[:, b4 * 16:(b4 + 1) * 16, 0:64], vv, ["PTM", "VM"], [f"aV{b_}"])

        loads(0)
        gidx = 0
        ecnt = 0
        for hi, H in enumerate(heads):
            if hi + 1 < len(heads):
                loads(hi + 1)
            b_ = hi % 2
            QT, KT_, VA = QTs[b_], KTs[b_], VAs[b_]
            kQ, kK, kV = f"aQ{b_}", f"aK{b_}", f"aV{b_}"
            Cq, scale, bh = H["Cq"], H["scale"], H["bias_head"]
            items = [(qt, kp) for qt in range(NTT) for kp in range(2 * (qt + 1))]

            def emit_qk_exp(qt, kp, g):
                si = g % 2
                psA, psB = C.ps[2 * si], C.ps[2 * si + 1]
                pk2 = [f"ps{2 * si}", f"ps{2 * si + 1}"]
                pp, ppk = P[g % 3], f"aP{g % 3}"
                qs = slice(qt * TT, (qt + 1) * TT)
                def mm(h):
                    for m, pst in ((0, psA), (1, psB)):
                        kb = 2 * kp + m
                        ins = h.matmul(pst[:], KT_[:, kb * 128:(kb + 1) * 128], QT[:, qs], start=True, stop=True)
                    return ins
                sc.op("pe", [kQ, kK], pk2, mm)
                if bias is None and 2 * kp + 1 < 4 * qt:
                    both = C.psall[:, si * 1024:(si + 1) * 1024]
                    sc.op("act", pk2, [ppk], lambda h, both=both: h.activation(out=pp[:, 0:1024], in_=both, func=AF.Exp, scale=scale))
                    return
                for m, pst in ((0, psA), (1, psB)):
                    kb = 2 * kp + m
                    src_, srck = pst, pk2[m]
                    if kb >= 4 * qt:
                        mi = kb - 4 * qt
                        dt_, dk = dtmp[m], f"aD{m}"
                        sc.op("dve", [pk2[m], "amadd"], [dk], lambda h, pst=pst, mi=mi, dt_=dt_: h.tensor_tensor(
                            out=dt_[:], in0=pst[:], in1=C.amadd[:, mi, :], op=ALU.add))
                        src_, srck = dt_, dk
                    if bias is None:
                        sc.op("act", [srck], [ppk], lambda h, m=m, src_=src_: h.activation(out=pp[:, m * TT:(m + 1) * TT], in_=src_[:], func=AF.Exp, scale=scale))
                    else:
                        sc.op("act", [srck, "aB"], [ppk], lambda h, m=m, src_=src_, kb=kb: h.activation(
                            out=pp[:, m * TT:(m + 1) * TT], in_=src_[:], func=AF.Exp, scale=scale, bias=bias[:, kb, bh:bh + 1]))

            def emit_pv(qt, kp, g):
                nonlocal ecnt
                nkb = 4 * (qt + 1)
                po, pok = C.ps[4 + qt % 2], f"ps{4 + qt % 2}"
                pp, ppk = P[g % 3], f"aP{g % 3}"
                def pv(h):
                    for m in (0, 1):
                        kb = 2 * kp + m
                        ins = h.matmul(po[:, :], VA[:, kb, :], pp[:, m * TT:(m + 1) * TT], start=(kb == 0), stop=(kb == nkb - 1))
                    return ins
                sc.op("pe", [ppk, kV], [pok], pv)
                if kp == 2 * (qt + 1) - 1:
                    i = ecnt % 2
                    ecnt += 1
                    qs = slice(qt * TT, (qt + 1) * TT)
                    o_, r_, y_ = osb[i], rec[i], yb[i]
                    sc.op("act", [pok], [f"aO{i}"], lambda h: h.activation(out=o_[:], in_=po[0:65, :], func=AF.Identity))
                    pb, pbk = C.ps[6 + i], f"ps{6 + i}"
                    sc.op("pe", [f"aO{i}", "onesf"], [pbk], lambda h: h.matmul(pb[0:64, :], C.onesf[64:65, 0:64], o_[64:65, :], start=True, stop=True))
                    sc.op("dve", [pbk], [f"aR{i}"], lambda h: h.reciprocal(out=r_[:], in_=pb[0:64, :]))
                    sc.op("dve", [f"aR{i}", f"aO{i}"], [f"aY{i}"], lambda h: h.tensor_tensor(out=y_[:], in0=o_[0:64, :], in1=r_[:], op=ALU.mult))
                    sc.dma("sp", S["Y"][H["yrow"]:H["yrow"] + 64, qs], y_[:], [f"aY{i}"], ["Y"])

            prev = None
            for (qt, kp) in items:
                emit_qk_exp(qt, kp, gidx)
                if prev is not None:
                    emit_pv(*prev)
                prev = (qt, kp, gidx)
                gidx += 1
            emit_pv(*prev)
    sc.barrier()


def go(C, L, n):
    return L * 100 + n <= C.limit


def layer(C, I, S, L):
    sc, nc = C.sc, C.nc
    V = C.vecs
    if not go(C, L, 1):
        return
    with ExitStack() as ph:
        C.ph = ph
        rmsnorm_fm(C, S["H"], 0, 8, V, ["vecs"], S["HN"], 0, BF16, 1024, gcol0=0)
    sc.barrier()
    if not go(C, L, 2):
        return
    with ExitStack() as ph:
        C.ph = ph
        def dst_of(j):
            if j >= 15:
                return (S["PF32"], (j - 15) * 128 - j * 128, F32)
            return (S["PF"], 0, BF16)
        ev = store_evac(C, lambda j: ((S["PF32"], (j - 15) * 128, F32) if j >= 15 else (S["PF"], j * 128, BF16)))
        linear_fm(C, S["HN"], 0, 8, I["w_in_fm"][L], NIN_FM, 2048, ev)
    sc.barrier()
    if not go(C, L, 3):
        return
    with ExitStack() as ph:
        C.ph = ph
        linear_tm(C, S["HN"], 8, I["w_in_tm"][L], 512, S["PTM"])
    sc.barrier()
    if not go(C, L, 4):
        return
    with ExitStack() as ph:
        C.ph = ph
        rmsnorm_fm(C, S["PF"], 0, 2, V, ["vecs"], S["CQN"], 0, BF16, 256, gcol0=32, tag="rq")
    sc.barrier()
    with ExitStack() as ph:
        C.ph = ph
        rmsnorm_fm(C, S["PF"], 256, 1, V, ["vecs"], S["KVN"], 0, BF16, 128, gcol0=34, tag="rk")
    sc.barrier()
    if not go(C, L, 6):
        return
    mla_proj(C, I, S, L)
    with ExitStack() as ph:
        C.ph = ph
        linear_tm(C, S["KVN"], 1, I["w_uv"][L], 256, S["VM"])
    sc.barrier()
    if not go(C, L, 8):
        return
    attention_multi(C, S, [dict(q=[(S["QA"][h, 0:96, :], 0, 96)], k=[(S["KA"][h, 0:96, :], 0, 96)], kones=None, Cq=96,
                                v=S["VM"], vcol=64 * h, bias_head=None, scale=96 ** -0.5, yrow=64 * h) for h in range(4)], False)
    if not go(C, L, 9):
        return
    fox_prep(C, I, S, L)
    if not go(C, L, 10):
        return
    attention_multi(C, S, [dict(q=[(S["PF"][640 + 64 * h:640 + 64 * h + 64, :], 0, 64), (S["FAUG"][h, 0:3, :], 64, 3)],
                                k=[(S["PF"][896 + 64 * h:896 + 64 * h + 64, :], 0, 64), (S["FAUG"][h, 3:6, :], 67, 3)], kones=(64, 6), Cq=70,
                                v=S["PTM"], vcol=64 * h, bias_head=None, scale=0.125, yrow=256 + 64 * h) for h in range(4)], False)
    if not go(C, L, 11):
        return
    hgrn2(C, I, S, L)
    if not go(C, L, 12):
        return
    s5(C, I, S, L)
    if not go(C, L, 13):
        return
    with ExitStack() as ph:
        C.ph = ph
        sg = [psb(C, f"gsg{i}", [128, 2, TT], BF16) for i in range(2)]
        def post_c(o, ko, t):
            i = t % 2
            sc.dma("pool", sg[i][:], S["PF"][1408:1664, t * TT:(t + 1) * TT].rearrange("(k p) n -> p k n", p=128), ["PF"], [f"gsg{i}"])
            sc.op("act", [f"gsg{i}"], [f"gsg{i}"], lambda h: h.activation(out=sg[i][:], in_=sg[i][:], func=AF.Sigmoid))
            sc.op("dve", [f"gsg{i}", ko], [ko], lambda h: h.tensor_tensor(out=o[:], in0=o[:], in1=sg[i][:], op=ALU.mult))
        for g in range(4):
            rmsnorm_fm(C, S["Y"], 256 * g, 2, V, ["vecs"], S["YN"], 256 * g, BF16, 256, gcol0=24 + 2 * g,
                       post=(post_c if g == 2 else None), tag=f"gn{g}")
    sc.barrier()
    if not go(C, L, 14):
        return
    with ExitStack() as ph:
        C.ph = ph
        linear_fm(C, S["YN"], 0, 8, I["w_out"][L], 8, 2048, resid_evac(C, S))
    sc.barrier()
    if not go(C, L, 15):
        return
    with ExitStack() as ph:
        C.ph = ph
        rmsnorm_fm(C, S["H"], 0, 8, V, ["vecs"], S["HN"], 0, BF16, 1024, gcol0=8)
    sc.barrier()
    ffn_up(C, I, S, L)
    if not go(C, L, 17):
        return
    with ExitStack() as ph:
        C.ph = ph
        linear_fm(C, S["ACT"], 0, 22, I["w_down"][L], 8, 2048, resid_evac(C, S))
    sc.barrier()
    if not go(C, L, 18):
        return
    with ExitStack() as ph:
        C.ph = ph
        rmsnorm_fm(C, S["H"], 0, 8, V, ["vecs"], S["HN"], 0, BF16, 1024, gcol0=16)
    sc.barrier()
    ple(C, I, S, L)


def linear_tm(C, src, KT, W, N, dst):
    sc, nc = C.sc, C.nc
    w = psb(C, "ltw", [128, KT, N], BF16)
    sc.dma("pool", w[:], W.rearrange("(k p) n -> p k n", p=128), [], ["ltw"])
    a = [psb(C, f"lta{i}", [128, KT, TT], BF16) for i in range(2)]
    o = [psb(C, f"lto{i}", [128, N], BF16) for i in range(3)]
    cnt = 0
    for t in range(NTT):
        i = t % 2
        sv = src[0:KT * 128, t * TT:(t + 1) * TT].rearrange("(k p) n -> p k n", p=128)
        sc.dma("pool", a[i][:], sv, ["HN", "KVN"], [f"lta{i}"])
        for b in range(4):
            pi = cnt % 4
            oi = cnt % 3
            cnt += 1
            pst, pk = C.ps[pi], f"ps{pi}"
            def mm(h, b=b, pst=pst, i=i):
                for k in range(KT):
                    ins = h.matmul(pst[:, 0:N], a[i][:, k, b * 128:(b + 1) * 128], w[:, k, :], start=(k == 0), stop=(k == KT - 1))
                return ins
            sc.op("pe", [f"lta{i}", "ltw"], [pk], mm)
            if cnt % 2 == 0:
                sc.op("act", [pk], [f"lto{oi}"], lambda h, pst=pst, oi=oi: h.activation(out=o[oi][:], in_=pst[:, 0:N], func=AF.Identity))
            else:
                sc.op("dve", [pk], [f"lto{oi}"], lambda h, pst=pst, oi=oi: h.tensor_copy(out=o[oi][:], in_=pst[:, 0:N]))
            r0 = t * TT + b * 128
            sc.dma("sp", dst[r0:r0 + 128, :], o[oi][:], [f"lto{oi}"], ["PTM", "VM"])


def mla_proj(C, I, S, L):
    sc, nc = C.sc, C.nc
    with ExitStack() as ph:
        C.ph = ph
        wq = psb(C, "mwq", [128, 2, 1024], BF16)
        wk = psb(C, "mwk", [128, 512], BF16)
        sc.dma("pool", wq[:], I["w_uq"][L].rearrange("(k p) n -> p k n", p=128), [], ["mwq"])
        sc.dma("pool", wk[:], I["w_uk"][L], [], ["mwk"])
        cq = [psb(C, f"mcq{i}", [128, 2, TT], BF16) for i in range(2)]
        kv = [psb(C, f"mkv{i}", [128, TT], BF16) for i in range(2)]
        kr = [psb(C, f"mkr{i}", [128, 2, TT], BF16) for i in range(2)]
        cc = [psb(C, f"mcc{i}", [128, TT]) for i in range(2)]
        ss = [psb(C, f"mss{i}", [128, TT]) for i in range(2)]
        stq = [psb(C, f"msq{i}", [128, TT], BF16) for i in range(3)]
        t1 = [psb(C, f"mt1{i}", [128, TT]) for i in range(2)]
        t2 = [psb(C, f"mt2{i}", [128, TT]) for i in range(2)]
        kpe = [psb(C, f"mkp{i}", [128, TT], BF16) for i in range(2)]
        cnt = 0
        for t in range(NTT):
            i = t % 2
            cs = slice(t * TT, (t + 1) * TT)
            sc.dma("pool", cq[i][:], S["CQN"][:, cs].rearrange("(k p) n -> p k n", p=128), ["CQN"], [f"mcq{i}"])
            sc.dma("pool", kv[i][:], S["KVN"][:, cs], ["KVN"], [f"mkv{i}"])
            sc.dma("pool", kr[i][:], S["PF"][384:640, cs].rearrange("(k p) n -> p k n", p=128), ["PF"], [f"mkr{i}"])
            sc.dma("pool", cc[i][64:96, :], S["ROPE"][0, :, cs], ["ROPE"], [f"mcc{i}"])
            sc.dma("pool", ss[i][64:96, :], S["ROPE"][1, :, cs], ["ROPE"], [f"mss{i}"])
            R = slice(64, 96)
            sc.op("dve", [f"mkr{i}", f"mcc{i}"], [f"mt1{i}"], lambda h: h.tensor_tensor(out=t1[i][R, :], in0=kr[i][R, 0, :], in1=cc[i][R, :], op=ALU.mult))
            sc.op("dve", [f"mkr{i}", f"mss{i}"], [f"mt2{i}"], lambda h: h.tensor_tensor(out=t2[i][R, :], in0=kr[i][R, 1, :], in1=ss[i][R, :], op=ALU.mult))
            sc.op("dve", [f"mt1{i}", f"mt2{i}"], [f"mkp{i}"], lambda h: h.tensor_tensor(out=kpe[i][R, :], in0=t1[i][R, :], in1=t2[i][R, :], op=ALU.add))
            for hd in range(4):
                sc.dma("sp", S["KA"][hd, 64:96, cs], kpe[i][R, :], [f"mkp{i}"], ["KA"])
            for hd in range(4):
                pi = cnt % 4
                si = cnt % 3
                cnt += 1
                pst, pk = C.ps[pi], f"ps{pi}"
                sc.op("pe", [f"mkv{i}", "mwk"], [pk], lambda h, pst=pst, hd=hd: h.matmul(pst[:], wk[:, hd * 128:(hd + 1) * 128], kv[i][:], start=True, stop=True))
                sc.op("act", [pk], [f"msq{si}"], lambda h, pst=pst, si=si: h.activation(out=stq[si][0:64, :], in_=pst[0:64, :], func=AF.Identity))
                sc.dma("sp", S["KA"][hd, 0:64, cs], stq[si][0:64, :], [f"msq{si}"], ["KA"])
                pa, pb_ = cnt % 4, (cnt + 1) % 4
                si = cnt % 3
                cnt += 2
                psA, psB = C.ps[pa], C.ps[pb_]
                def mmq(h, hd=hd, psA=psA, psB=psB):
                    for which, pst in ((0, psA), (1, psB)):
                        c0 = (2 * hd + which) * 128
                        for k in range(2):
                            ins = h.matmul(pst[:], wq[:, k, c0:c0 + 128], cq[i][:, k, :], start=(k == 0), stop=(k == 1))
                    return ins
                sc.op("pe", [f"mcq{i}", "mwq"], [f"ps{pa}", f"ps{pb_}"], mmq)
                sq_ = stq[si]
                sc.op("act", [f"ps{pa}"], [f"msq{si}"], lambda h, psA=psA, sq_=sq_: h.activation(out=sq_[0:64, :], in_=psA[0:64, :], func=AF.Identity))
                sc.op("dve", [f"ps{pa}", f"mcc{i}"], [f"mt1{i}"], lambda h, psA=psA: h.tensor_tensor(out=t1[i][R, :], in0=psA[R, :], in1=cc[i][R, :], op=ALU.mult))
                sc.op("dve", [f"ps{pb_}", f"mss{i}"], [f"mt2{i}"], lambda h, psB=psB: h.tensor_tensor(out=t2[i][R, :], in0=psB[R, :], in1=ss[i][R, :], op=ALU.mult))
                sc.op("dve", [f"mt1{i}", f"mt2{i}"], [f"msq{si}"], lambda h, sq_=sq_: h.tensor_tensor(out=sq_[R, :], in0=t1[i][R, :], in1=t2[i][R, :], op=ALU.add))
                sc.dma("sp", S["QA"][hd, 0:96, cs], sq_[0:96, :], [f"msq{si}"], ["QA"])
    sc.barrier()


def fox_prep(C, I, S, L):
    sc, nc = C.sc, C.nc
    with ExitStack() as ph:
        C.ph = ph
        z = psb(C, "fz", [4, NT])
        cum = psb(C, "fcum", [4, NT])
        hi = psb(C, "fhi", [4, NT], BF16)
        mid = psb(C, "fmid", [4, NT], BF16)
        lo = psb(C, "flo", [4, NT], BF16)
        r1 = psb(C, "fr1", [4, NT])
        bt = psb(C, "fbt", [128, 256])
        sc.dma("sp", z[:], S["PF32"][256:260, :], ["PF32"], ["fz"])
        sc.op("act", ["fz", "vecs"], ["fz"], lambda h: h.activation(out=z[:], in_=z[:], func=AF.Sigmoid, bias=C.vecs[0:4, 35:36]))
        sc.op("act", ["fz"], ["fz"], lambda h: h.activation(out=z[:], in_=z[:], func=AF.Ln))
        for g in range(4):
            cs = slice(g * 2048, (g + 1) * 2048)
            init = 0.0 if g == 0 else cum[:, g * 2048 - 1:g * 2048]
            sc.op("dve", ["fz", "onesf", "fcum"], ["fcum"], lambda h, cs=cs, init=init: h.tensor_tensor_scan(
                out=cum[:, cs], data0=C.onesf[0:4, :], data1=z[:, cs], initial=init, op0=ALU.mult, op1=ALU.add))
        pst = C.ps[0]
        def tr(h):
            for b in range(64):
                ins = h.transpose(pst[:, 4 * b:4 * b + 4], cum[0:4, b * 128:(b + 1) * 128], C.ident[0:4, 0:4])
            return ins
        sc.op("pe", ["fcum", "ident"], ["ps0"], tr)
        sc.op("act", ["ps0"], ["fbt"], lambda h: h.activation(out=bt[:], in_=pst[:, 0:256], func=AF.Identity, scale=-1.0))
        sc.dma("sp", S["FBIAS"].rearrange("p b h -> p (b h)"), bt[:], ["fbt"], ["FBIAS"])
        sc.op("dve", ["fcum"], ["fr1"], lambda h: h.tensor_scalar(out=r1[:], in0=cum[:], scalar1=8.0, scalar2=None, op0=ALU.mult))
        sc.op("dve", ["fr1"], ["fhi"], lambda h: h.tensor_copy(out=hi[:], in_=r1[:]))
        sc.op("dve", ["fr1", "fhi"], ["fr1"], lambda h: h.tensor_tensor(out=r1[:], in0=r1[:], in1=hi[:], op=ALU.subtract))
        sc.op("dve", ["fr1"], ["fmid"], lambda h: h.tensor_copy(out=mid[:], in_=r1[:]))
        sc.op("dve", ["fr1", "fmid"], ["fr1"], lambda h: h.tensor_tensor(out=r1[:], in0=r1[:], in1=mid[:], op=ALU.subtract))
        sc.op("dve", ["fr1"], ["flo"], lambda h: h.tensor_copy(out=lo[:], in_=r1[:]))
        for hd in range(4):
            for j, tl, k in ((0, hi, "fhi"), (1, mid, "fmid"), (2, lo, "flo")):
                sc.dma("sp", S["FAUG"][hd, j:j + 1, :], tl[hd:hd + 1, :], [k], ["FAUG"])
        for tl, k in ((hi, "fhi"), (mid, "fmid"), (lo, "flo")):
            sc.op("dve", [k], [k], lambda h, tl=tl: h.tensor_scalar(out=tl[:], in0=tl[:], scalar1=-1.0, scalar2=None, op0=ALU.mult))
        for hd in range(4):
            for j, tl, k in ((0, hi, "fhi"), (1, mid, "fmid"), (2, lo, "flo")):
                sc.dma("sp", S["FAUG"][hd, 3 + j:4 + j, :], tl[hd:hd + 1, :], [k], ["FAUG"])
    sc.barrier()


def hgrn2(C, I, S, L):
    sc, nc = C.sc, C.nc
    SCW = 2048
    CH = 64
    NCH = SCW // CH
    MID, LAST = CH // 2 - 1, CH - 1
    with ExitStack() as ph:
        C.ph = ph
        lbt = psb(C, "hlb", [128, 2, 2])
        lb = psb(C, "hlbv", [128, 2])
        oml = psb(C, "homl", [128, 2])
        rm = psb(C, "hrm", [128, SCW])
        sc.dma("sp", lbt[:], I["lbp"], [], ["hlb"])
        if L == 0:
            sc.op("dve", [], ["hlbv"], lambda h: h.memset(lb[:], 0.0))
        else:
            sc.op("dve", ["hlb"], ["hlbv"], lambda h: h.tensor_tensor(out=lb[:], in0=lbt[:, :, 1], in1=lbt[:, :, 0], op=ALU.subtract))
            sc.op("act", ["hlbv"], ["hlbv"], lambda h: h.activation(out=lb[:], in_=lb[:], func=AF.Sigmoid))
        sc.op("dve", ["hlbv"], ["homl"], lambda h: h.tensor_scalar(out=oml[:], in0=lb[:], scalar1=-1.0, scalar2=1.0, op0=ALU.mult, op1=ALU.add))
        sc.op("dve", [], ["hrm"], lambda h: h.memset(rm[:], 1.0))
        sc.op("dve", ["hrm"], ["hrm"], lambda h: h.memset(rm[:].rearrange("p (c j) -> p c j", j=CH)[:, :, 0:1], 0.0))
        z = psb(C, "hz", [128, SCW]); f = psb(C, "hf", [128, SCW]); kkf = psb(C, "hkk", [128, SCW])
        b = psb(C, "hb", [128, SCW]); nb = psb(C, "hnb", [128, SCW]); e1 = psb(C, "he1", [128, SCW])
        q = psb(C, "hq", [128, SCW], BF16)
        qa = psb(C, "hqa", [128, SCW], BF16); ka = psb(C, "hka", [128, SCW], BF16)
        qt_ = psb(C, "hqt", [128, SCW], BF16); kl = psb(C, "hkl", [128, SCW], BF16)
        dd = psb(C, "hd", [128, NCH])
        it = psb(C, "hit", [CH, NCH, 128], BF16)
        klT = [psb(C, f"hklT{i}", [CH, 128], BF16) for i in range(2)]
        scb = [psb(C, f"hsc{i}", [CH, CH], BF16) for i in range(2)]
        Sf = [psb(C, f"hSf{i}", [128, 64]) for i in range(2)]
        Sb = [psb(C, f"hSb{i}", [128, 64], BF16) for i in range(2)]
        tmp = [psb(C, f"htmp{i}", [128, 64]) for i in range(2)]
        ys = [psb(C, f"hys{i}", [64, SCW], BF16) for i in range(2)]
        for pair in range(2):
            for hh in range(2):
                sc.op("dve", [], [f"hSf{hh}"], lambda h, hh=hh: h.memset(Sf[hh][:], 0.0))
                sc.op("dve", [], [f"hSb{hh}"], lambda h, hh=hh: h.memset(Sb[hh][:], 0.0))
            for s_ in range(NT // SCW):
                cs = slice(s_ * SCW, (s_ + 1) * SCW)
                sc.dma("pool", z[:], S["PF32"][128 * pair:128 * pair + 128, cs], ["PF32"], ["hz"])
                sc.dma("pool", q[:], S["PF"][1152 + 128 * pair:1152 + 128 * pair + 128, cs], ["PF"], ["hq"])
                sc.dma("pool", it[:], S["PTM"][cs, 256 + 128 * pair:256 + 128 * pair + 128].rearrange("(c p) d -> p c d", p=CH), ["PTM"], ["hit"])
                sc.op("act", ["hz"], ["hz"], lambda h: h.activation(out=z[:], in_=z[:], func=AF.Sigmoid))
                sc.op("dve", ["hz", "hlbv", "homl"], ["hf"], lambda h: h.tensor_scalar(out=f[:], in0=z[:], scalar1=oml[:, pair:pair + 1], scalar2=lb[:, pair:pair + 1], op0=ALU.mult, op1=ALU.add))
                sc.op("dve", ["hf"], ["hkk"], lambda h: h.tensor_scalar(out=kkf[:], in0=f[:], scalar1=-1.0, scalar2=1.0, op0=ALU.mult, op1=ALU.add))
                sc.op("act", ["hf"], ["hf"], lambda h: h.activation(out=f[:], in_=f[:], func=AF.Ln))
                sc.op("dve", ["hf", "hrm"], ["hb"], lambda h: h.tensor_tensor_scan(out=b[:], data0=rm[:], data1=f[:], initial=0.0, op0=ALU.mult, op1=ALU.add))
                sc.op("dve", ["hb"], ["hnb"], lambda h: h.tensor_scalar(out=nb[:], in0=b[:], scalar1=-1.0, scalar2=None, op0=ALU.mult))
                b3 = b[:].rearrange("p (c j) -> p c j", j=CH)
                sc.op("act", ["hb"], ["hd"], lambda h: h.activation(out=dd[:], in_=b3[:, :, LAST], func=AF.Exp))
                sc.op("act", ["hb"], ["he1"], lambda h: h.activation(out=e1[:], in_=b[:], func=AF.Exp))
                sc.op("dve", ["he1", "hq"], ["hqt"], lambda h: h.tensor_tensor(out=qt_[:], in0=q[:], in1=e1[:], op=ALU.mult))
                for c in range(NCH):
                    cc_ = slice(c * CH, (c + 1) * CH)
                    sc.op("act", ["hb", "hnb", "hqt"], ["he1"], lambda h, c=c, cc_=cc_: h.activation(out=e1[:, cc_], in_=b[:, cc_], func=AF.Exp, bias=nb[:, c * CH + MID:c * CH + MID + 1]))
                sc.op("dve", ["he1", "hq"], ["hqa"], lambda h: h.tensor_tensor(out=qa[:], in0=q[:], in1=e1[:], op=ALU.mult))
                for c in range(NCH):
                    cc_ = slice(c * CH, (c + 1) * CH)
                    sc.op("act", ["hb", "hqa"], ["he1"], lambda h, c=c, cc_=cc_: h.activation(out=e1[:, cc_], in_=b[:, cc_], func=AF.Exp, scale=-1.0, bias=b[:, c * CH + MID:c * CH + MID + 1]))
                sc.op("dve", ["he1", "hkk"], ["hka"], lambda h: h.tensor_tensor(out=ka[:], in0=kkf[:], in1=e1[:], op=ALU.mult))
                for c in range(NCH):
                    cc_ = slice(c * CH, (c + 1) * CH)
                    sc.op("act", ["hb", "hka"], ["he1"], lambda h, c=c, cc_=cc_: h.activation(out=e1[:, cc_], in_=b[:, cc_], func=AF.Exp, scale=-1.0, bias=b[:, c * CH + LAST:c * CH + LAST + 1]))
                sc.op("dve", ["he1", "hkk"], ["hkl"], lambda h: h.tensor_tensor(out=kl[:], in0=kkf[:], in1=e1[:], op=ALU.mult))
                for c in range(NCH):
                    cc_ = slice(c * CH, (c + 1) * CH)
                    ti = c % 2
                    pstT, pkT = C.ps[6], "ps6"
                    sc.op("pe", ["hkl", "identb"], [pkT], lambda h, cc_=cc_: h.matmul(pstT[0:CH, 0:128], kl[:, cc_], C.identb[:], start=True, stop=True))
                    sc.op("act", [pkT], [f"hklT{ti}"], lambda h, ti=ti: h.activation(out=klT[ti][:], in_=pstT[0:CH, 0:128], func=AF.Identity))
                    for hh in range(2):
                        Pr = slice(64 * hh, 64 * hh + 64)
                        hcol = slice(64 * hh, 64 * hh + 64)
                        pS_, pSk = C.ps[hh], f"ps{hh}"
                        sc.op("pe", ["hka", "hqa"], [pSk], lambda h, Pr=Pr, cc_=cc_, pS_=pS_: h.matmul(pS_[0:CH, 0:CH], ka[Pr, cc_], qa[Pr, cc_], start=True, stop=True))
                        sc.op("dve", [pSk, "amask"], [f"hsc{hh}"], lambda h, hh=hh, pS_=pS_: h.scalar_tensor_tensor(
                            out=scb[hh][:], in0=pS_[0:CH, 0:CH], scalar=1e30, in1=C.amask[0:CH, 0, 0:CH], op0=ALU.min, op1=ALU.mult))
                        po, pok = C.ps[2 + hh], f"ps{2 + hh}"
                        def mo(h, hh=hh, c=c, cc_=cc_, hcol=hcol, po=po):
                            h.matmul(po[0:64, 0:CH], it[:, c, hcol], scb[hh][:], start=True, stop=False)
                            return h.matmul(po[0:64, 0:CH], Sb[hh][:], qt_[:, cc_], start=False, stop=True)
                        sc.op("pe", [f"hsc{hh}", "hit", f"hSb{hh}", "hqt"], [pok], mo)
                        sc.op("act", [pok], [f"hys{hh}"], lambda h, hh=hh, cc_=cc_, po=po: h.activation(out=ys[hh][:, cc_], in_=po[0:64, 0:CH], func=AF.Identity))
                        pd, pdk = C.ps[4 + hh], f"ps{4 + hh}"
                        sc.op("pe", [f"hklT{ti}", "hit"], [pdk], lambda h, ti=ti, c=c, hcol=hcol, pd=pd: h.matmul(pd[:, 0:64], klT[ti][:], it[:, c, hcol], start=True, stop=True))
                        sc.op("dve", [f"hSf{hh}", "hd"], [f"htmp{hh}"], lambda h, hh=hh, c=c: h.tensor_scalar(out=tmp[hh][:], in0=Sf[hh][:], scalar1=dd[:, c:c + 1], scalar2=None, op0=ALU.mult))
                        sc.op("dve", [pdk, f"htmp{hh}", "cvec"], [f"hSf{hh}"], lambda h, hh=hh, pd=pd: h.scalar_tensor_tensor(
                            out=Sf[hh][:], in0=pd[:, 0:64], scalar=C.cvec[:, 2 + 2 * hh:3 + 2 * hh], in1=tmp[hh][:], op0=ALU.mult, op1=ALU.add))
                        sc.op("act", [f"hSf{hh}"], [f"hSb{hh}"], lambda h, hh=hh: h.activation(out=Sb[hh][:], in_=Sf[hh][:], func=AF.Identity))
                for hh in range(2):
                    r0 = 512 + 128 * pair + 64 * hh
                    sc.dma("sp", S["Y"][r0:r0 + 64, cs], ys[hh][:], [f"hys{hh}"], ["Y"])
    sc.barrier()


S5_SHIFTS = [(1, 4), (4, 4), (16, 4), (64, 4), (256, 4), (1024, 4), (4096, 2)]


def s5(C, I, S, L):
    sc, nc = C.sc, C.nc
    with ExitStack() as ph:
        C.ph = ph
        pa = psb(C, "s5a", [128, 3, 16])
        sc.dma("sp", pa[:], I["s5a"][L], [], ["s5a"])
        stp = psb(C, "s5stp", [128, 16]); lre = psb(C, "s5lre", [128, 16]); mag = psb(C, "s5mag", [128, 16])
        ang = psb(C, "s5ang", [128, 16]); ki = psb(C, "s5ki", [128, 16], I32); kf = psb(C, "s5kf", [128, 16])
        sn = psb(C, "s5sn", [128, 16]); cs_ = psb(C, "s5cs", [128, 16])
        NP = 19
        pre = psb(C, "s5pre", [128, NP + 1, 16]); pim = psb(C, "s5pim", [128, NP + 1, 16])
        t1 = psb(C, "s5t1", [128, 16]); t2 = psb(C, "s5t2", [128, 16])
        v1 = psb(C, "s5v1", [128, NP + 1, 16]); v2 = psb(C, "s5v2", [128, NP + 1, 16])
        K = ["s5p"]
        def dv(fn, r=K, w=K):
            sc.op("dve", r, w, fn)
        def av(fn, r=K, w=K):
            sc.op("act", r, w, fn)
        sc.op("act", ["s5a"], K, lambda h: h.activation(out=stp[:], in_=pa[:, 2, :], func=AF.Exp))
        dv(lambda h: h.tensor_scalar(out=lre[:], in0=pa[:, 0, :], scalar1=-1e-4, scalar2=None, op0=ALU.min), r=K + ["s5a"])
        dv(lambda h: h.tensor_tensor(out=t1[:], in0=lre[:], in1=stp[:], op=ALU.mult))
        av(lambda h: h.activation(out=mag[:], in_=t1[:], func=AF.Exp))
        dv(lambda h: h.tensor_tensor(out=ang[:], in0=pa[:, 1, :], in1=stp[:], op=ALU.mult), r=K + ["s5a"])
        def sincos(dst, shift):
            dv(lambda h: h.tensor_scalar(out=t1[:], in0=ang[:], scalar1=shift, scalar2=None, op0=ALU.add))
            dv(lambda h: h.tensor_scalar(out=ki[:], in0=t1[:], scalar1=1.0 / TWO_PI, scalar2=None, op0=ALU.mult))
            dv(lambda h: h.tensor_copy(out=kf[:], in_=ki[:]))
            dv(lambda h: h.scalar_tensor_tensor(out=t1[:], in0=kf[:], scalar=-TWO_PI, in1=t1[:], op0=ALU.mult, op1=ALU.add))
            dv(lambda h: h.tensor_scalar(out=t1[:], in0=t1[:], scalar1=3.1415925, scalar2=-3.1415925, op0=ALU.min, op1=ALU.max))
            av(lambda h: h.activation(out=dst[:], in_=t1[:], func=AF.Sin))
        sincos(sn, 0.0)
        sincos(cs_, math.pi / 2)
        dv(lambda h: h.tensor_tensor(out=pre[:, 0, :], in0=mag[:], in1=cs_[:], op=ALU.mult))
        dv(lambda h: h.tensor_tensor(out=pim[:, 0, :], in0=mag[:], in1=sn[:], op=ALU.mult))
        exps = []
        for s_, r_ in S5_SHIFTS:
            for m in range(1, r_):
                exps.append(s_ * m)
        idx = {1: 0}
        def cmul(dst, a, b_):
            dv(lambda h: h.tensor_tensor(out=t1[:], in0=pre[:, a, :], in1=pre[:, b_, :], op=ALU.mult))
            dv(lambda h: h.tensor_tensor(out=t2[:], in0=pim[:, a, :], in1=pim[:, b_, :], op=ALU.mult))
            dv(lambda h: h.tensor_tensor(out=pre[:, dst, :], in0=t1[:], in1=t2[:], op=ALU.subtract))
            dv(lambda h: h.tensor_tensor(out=t1[:], in0=pre[:, a, :], in1=pim[:, b_, :], op=ALU.mult))
            dv(lambda h: h.tensor_tensor(out=t2[:], in0=pim[:, a, :], in1=pre[:, b_, :], op=ALU.mult))
            dv(lambda h: h.tensor_tensor(out=pim[:, dst, :], in0=t1[:], in1=t2[:], op=ALU.add))
        nxt = 1
        for e in exps:
            if e in idx:
                continue
            lowbit = e & (-e)
            base = 1
            while base * 4 <= e:
                base *= 4
            if e == base and e != 1:
                cmul(nxt, idx[base // 2], idx[base // 2])
            elif e == 2 * base:
                cmul(nxt, idx[base], idx[base])
            else:
                cmul(nxt, idx[2 * base], idx[base])
            idx[e] = nxt
            nxt += 1
        assert nxt <= NP + 1, nxt
        dv(lambda h: h.tensor_scalar(out=v1[:], in0=pre[:], scalar1=C.cvec[:, 2:3], scalar2=None, op0=ALU.mult), r=K + ["cvec"])
        dv(lambda h: h.scalar_tensor_tensor(out=v1[:], in0=pim[:], scalar=C.cvec[:, 3:4], in1=v1[:], op0=ALU.mult, op1=ALU.add), r=K + ["cvec"])
        dv(lambda h: h.tensor_scalar(out=v2[:], in0=pim[:], scalar1=C.cvec[:, 2:3], scalar2=None, op0=ALU.mult), r=K + ["cvec"])
        dv(lambda h: h.scalar_tensor_tensor(out=v2[:], in0=pre[:], scalar=C.cvec[:, 4:5], in1=v2[:], op0=ALU.mult, op1=ALU.add), r=K + ["cvec"])
        E = psb(C, "s5E", [128, 64])
        dv(lambda h: h.tensor_tensor(out=E[:], in0=C.ident[:, 0:64], in1=C.ident[:, 64:128], op=ALU.add), r=["ident"], w=["s5E"])
        pb = psb(C, "s5b", [128, 2, 5, 64])
        sc.dma("sp", pb[:], I["s5b"][L], [], ["s5b"])
        KB = ["s5q"]
        def dvb(fn, r=KB, w=KB):
            sc.op("dve", r, w, fn)
        def avb(fn):
            sc.op("act", KB, KB, fn)
        sh = [128, 2, 64]
        bst = psb(C, "b_st", sh); blr = psb(C, "b_lr", sh); bmg = psb(C, "b_mg", sh); ban = psb(C, "b_an", sh)
        bt1 = psb(C, "b_t1", sh); bt2 = psb(C, "b_t2", sh); bki = psb(C, "b_ki", sh, I32); bkf = psb(C, "b_kf", sh)
        bsn = psb(C, "b_sn", sh); bcs = psb(C, "b_cs", sh); bar = psb(C, "b_ar", sh); bai = psb(C, "b_ai", sh)
        bden = psb(C, "b_den", sh); bcr = psb(C, "b_cr", sh); bci = psb(C, "b_ci", sh)
        bbT = psb(C, "b_bbT", [128, 2, 128])
        sc.op("act", ["s5b"], KB, lambda h: h.activation(out=bst[:], in_=pb[:, :, 2, :], func=AF.Exp))
        dvb(lambda h: h.tensor_scalar(out=blr[:], in0=pb[:, :, 0, :], scalar1=-1e-4, scalar2=None, op0=ALU.min), r=KB + ["s5b"])
        dvb(lambda h: h.tensor_tensor(out=bt1[:], in0=blr[:], in1=bst[:], op=ALU.mult))
        avb(lambda h: h.activation(out=bmg[:], in_=bt1[:], func=AF.Exp))
        dvb(lambda h: h.tensor_tensor(out=ban[:], in0=pb[:, :, 1, :], in1=bst[:], op=ALU.mult), r=KB + ["s5b"])
        def sincos_b(dst, shift):
            dvb(lambda h: h.tensor_scalar(out=bt1[:], in0=ban[:], scalar1=shift, scalar2=None, op0=ALU.add))
            dvb(lambda h: h.tensor_scalar(out=bki[:], in0=bt1[:], scalar1=1.0 / TWO_PI, scalar2=None, op0=ALU.mult))
            dvb(lambda h: h.tensor_copy(out=bkf[:], in_=bki[:]))
            dvb(lambda h: h.scalar_tensor_tensor(out=bt1[:], in0=bkf[:], scalar=-TWO_PI, in1=bt1[:], op0=ALU.mult, op1=ALU.add))
            dvb(lambda h: h.tensor_scalar(out=bt1[:], in0=bt1[:], scalar1=3.1415925, scalar2=-3.1415925, op0=ALU.min, op1=ALU.max))
            avb(lambda h: h.activation(out=dst[:], in_=bt1[:], func=AF.Sin))
        sincos_b(bsn, 0.0)
        sincos_b(bcs, math.pi / 2)
        dvb(lambda h: h.tensor_tensor(out=bar[:], in0=bmg[:], in1=bcs[:], op=ALU.mult))
        dvb(lambda h: h.tensor_scalar(out=bar[:], in0=bar[:], scalar1=-1.0, scalar2=None, op0=ALU.add))
        dvb(lambda h: h.tensor_tensor(out=bai[:], in0=bmg[:], in1=bsn[:], op=ALU.mult))
        lim = pb[:, :, 1, :]
        dvb(lambda h: h.tensor_tensor(out=bt1[:], in0=blr[:], in1=blr[:], op=ALU.mult))
        dvb(lambda h: h.tensor_tensor(out=bt2[:], in0=lim, in1=lim, op=ALU.mult), r=KB + ["s5b"])
        dvb(lambda h: h.tensor_tensor(out=bden[:], in0=bt1[:], in1=bt2[:], op=ALU.add))
        dvb(lambda h: h.reciprocal(out=bden[:], in_=bden[:]))
        dvb(lambda h: h.tensor_tensor(out=bt1[:], in0=bar[:], in1=blr[:], op=ALU.mult))
        dvb(lambda h: h.tensor_tensor(out=bt2[:], in0=bai[:], in1=lim, op=ALU.mult), r=KB + ["s5b"])
        dvb(lambda h: h.tensor_tensor(out=bt1[:], in0=bt1[:], in1=bt2[:], op=ALU.add))
        dvb(lambda h: h.tensor_tensor(out=bcr[:], in0=bt1[:], in1=bden[:], op=ALU.mult))
        dvb(lambda h: h.tensor_tensor(out=bt1[:], in0=bai[:], in1=blr[:], op=ALU.mult))
        dvb(lambda h: h.tensor_tensor(out=bt2[:], in0=bar[:], in1=lim, op=ALU.mult), r=KB + ["s5b"])
        dvb(lambda h: h.tensor_tensor(out=bt1[:], in0=bt1[:], in1=bt2[:], op=ALU.subtract))
        dvb(lambda h: h.tensor_tensor(out=bci[:], in0=bt1[:], in1=bden[:], op=ALU.mult))
        bre_, bim_ = pb[:, :, 3, :], pb[:, :, 4, :]
        dvb(lambda h: h.tensor_tensor(out=bt1[:], in0=bcr[:], in1=bre_, op=ALU.mult), r=KB + ["s5b"])
        dvb(lambda h: h.tensor_tensor(out=bt2[:], in0=bci[:], in1=bim_, op=ALU.mult), r=KB + ["s5b"])
        dvb(lambda h: h.tensor_tensor(out=bbT[:, :, 0:64], in0=bt1[:], in1=bt2[:], op=ALU.subtract))
        dvb(lambda h: h.tensor_tensor(out=bt1[:], in0=bcr[:], in1=bim_, op=ALU.mult), r=KB + ["s5b"])
        dvb(lambda h: h.tensor_tensor(out=bt2[:], in0=bci[:], in1=bre_, op=ALU.mult), r=KB + ["s5b"])
        dvb(lambda h: h.tensor_tensor(out=bbT[:, :, 64:128], in0=bt1[:], in1=bt2[:], op=ALU.add))
        pc = psb(C, "s5c", [128, 16, 16])
        cb = psb(C, "s5cb", [128, 16, 16], BF16)
        sc.dma("sp", pc[:], I["s5c"][L], [], ["s5c"])
        sc.op("dve", ["s5c", "cvec"], ["s5cb"], lambda h: h.tensor_scalar(out=cb[:], in0=pc[:], scalar1=C.cvec[:, 5:6], scalar2=None, op0=ALU.mult))
        u = psb(C, "s5u", [128, 2, NT], BF16)
        sc.dma("sp", u[:], S["PF"][1664:1920, :].rearrange("(k p) n -> p k n", p=128), ["PF"], ["s5u"])
        X = [psb(C, f"s5X{i}", [128, NT], BF16) for i in range(2)]
        Am = [psb(C, f"s5A{i}", [128, NP, 128], BF16) for i in range(2)]
        bw = [psb(C, f"s5bw{i}", [128, 128], BF16) for i in range(2)]
        yst = [psb(C, f"s5y{i}", [16, TT]) for i in range(3)]
        ecnt = 0
        for g in range(16):
            gi = g % 2
            A, Ak = Am[gi], f"s5A{gi}"
            for e, j in idx.items():
                sc.op("dve", K + ["s5E"], [Ak], lambda h, j=j: h.tensor_scalar(out=A[:, j, 0:64], in0=E[:], scalar1=v1[:, j, g:g + 1], scalar2=None, op0=ALU.mult))
                sc.op("dve", K + ["s5E"], [Ak], lambda h, j=j: h.tensor_scalar(out=A[:, j, 64:128], in0=E[:], scalar1=v2[:, j, g:g + 1], scalar2=None, op0=ALU.mult))
            sc.op("dve", KB + ["cvec"], [f"s5bw{gi}"], lambda h: h.tensor_scalar(out=bw[gi][:], in0=bbT[:, g // 8, :], scalar1=C.cvec[:, 8 + g % 8:9 + g % 8], scalar2=None, op0=ALU.mult))
            cur = 0
            for t in range(NTT):
                pi = ecnt % 4
                pst, pk = C.ps[pi], f"ps{pi}"
                sc.op("pe", [f"s5bw{gi}", "s5u"], [pk], lambda h, pst=pst, t=t: h.matmul(pst[:], bw[gi][:], u[:, g // 8, t * TT:(t + 1) * TT], start=True, stop=True))
                ecnt += 1
                xk = f"s5X{cur}t{t}"
                if ecnt % 2:
                    sc.op("act", [pk], [xk], lambda h, pst=pst, t=t: h.activation(out=X[cur][:, t * TT:(t + 1) * TT], in_=pst[:], func=AF.Identity))
                else:
                    sc.op("dve", [pk], [xk], lambda h, pst=pst, t=t: h.tensor_copy(out=X[cur][:, t * TT:(t + 1) * TT], in_=pst[:]))
            for s_, r_ in S5_SHIFTS:
                nx = 1 - cur
                for t in range(NTT):
                    pi = ecnt % 4
                    ecnt += 1
                    pst, pk = C.ps[pi], f"ps{pi}"
                    t0 = t * TT
                    rk = {f"s5X{cur}t{t}"}
                    terms = [m for m in range(1, r_) if m * s_ < t0 + TT]
                    fast = len(terms) > 0 and all(m * s_ <= t0 for m in terms) and (t % 4 != 3)
                    def mm(h, t0=t0, pst=pst, s_=s_, r_=r_, cur=cur, terms=terms, fast=fast):
                        ins = None
                        if not fast:
                            ins = h.matmul(pst[:], C.identb[:], X[cur][:, t0:t0 + TT], start=True, stop=(len(terms) == 0))
                        for ii, m in enumerate(terms):
                            sh_ = m * s_
                            off = max(0, sh_ - t0)
                            ins = h.matmul(pst[:, off:TT], A[:, idx[sh_], :], X[cur][:, t0 + off - sh_:t0 + TT - sh_], start=(fast and ii == 0), stop=(ii == len(terms) - 1))
                        return ins
                    for m in range(1, r_):
                        sh_ = m * s_
                        if sh_ < t0 + TT:
                            lo_t = max(0, (t0 - sh_)) // TT
                            hi_t = (t0 + TT - 1 - sh_) // TT
                            for tt_ in range(lo_t, hi_t + 1):
                                rk.add(f"s5X{cur}t{tt_}")
                    sc.op("pe", sorted(rk) + [Ak, "identb"], [pk], mm)
                    xk = f"s5X{nx}t{t}"
                    if fast:
                        sc.op("dve", [pk, f"s5X{cur}t{t}"], [xk], lambda h, pst=pst, t0=t0, nx=nx, cur=cur: h.tensor_tensor(
                            out=X[nx][:, t0:t0 + TT], in0=pst[:], in1=X[cur][:, t0:t0 + TT], op=ALU.add))
                    else:
                        sc.op("act", [pk], [xk], lambda h, pst=pst, t0=t0, nx=nx: h.activation(out=X[nx][:, t0:t0 + TT], in_=pst[:], func=AF.Identity))
                cur = nx
            for t in range(NTT):
                pi = ecnt % 4
                yi = ecnt % 3
                ecnt += 1
                pst, pk = C.ps[pi], f"ps{pi}"
                sc.op("pe", [f"s5X{cur}t{t}", "s5cb"], [pk], lambda h, pst=pst, t=t, cur=cur: h.matmul(pst[0:16, :], cb[:, g, :], X[cur][:, t * TT:(t + 1) * TT], start=True, stop=True))
                sc.op("act", [pk], [f"s5y{yi}"], lambda h, pst=pst, yi=yi: h.activation(out=yst[yi][:], in_=pst[0:16, :], func=AF.Identity))
                sc.dma("sp", S["YS5"][16 * g:16 * g + 16, t * TT:(t + 1) * TT], yst[yi][:], [f"s5y{yi}"], ["YS5"])
    sc.barrier()
    with ExitStack() as ph:
        C.ph = ph
        V = C.vecs
        wg = psb(C, "gwg", [128, 2, 256], BF16)
        sc.dma("pool", wg[:], I["w_glu"][L].rearrange("(k p) n -> p k n", p=128), [], ["gwg"])
        yy = [psb(C, f"gy{i}", [128, 2, TT]) for i in range(2)]
        uu = [psb(C, f"gu{i}", [128, 2, TT], BF16) for i in range(2)]
        x2 = [psb(C, f"gx2{i}", [128, 2, TT]) for i in range(2)]
        za = [psb(C, f"gza{i}", [128, 2, TT]) for i in range(2)]
        zb = [psb(C, f"gzb{i}", [128, 2, TT], BF16) for i in range(2)]
        sg = [psb(C, f"gsg{i}", [128, TT]) for i in range(2)]
        ob = [psb(C, f"gob{i}", [128, 2, TT], BF16) for i in range(2)]
        for t in range(NTT):
            i = t % 2
            cs = slice(t * TT, (t + 1) * TT)
            ky, ku, kx, kz, kzb, ko = f"gy{i}", f"gu{i}", f"gx2{i}", f"gza{i}", f"gzb{i}", f"gob{i}"
            sc.dma("pool", yy[i][:], S["YS5"][:, cs].rearrange("(k p) n -> p k n", p=128), ["YS5"], [ky])
            sc.dma("pool", uu[i][:], S["PF"][1664:1920, cs].rearrange("(k p) n -> p k n", p=128), ["PF"], [ku])
            for k in range(2):
                sc.op("dve", [ky, ku, "vecs"], [ky], lambda h, k=k: h.scalar_tensor_tensor(out=yy[i][:, k, :], in0=uu[i][:, k, :], scalar=V[:, 36 + k:37 + k], in1=yy[i][:, k, :], op0=ALU.mult, op1=ALU.add))
            sc.op("act", [ky], [kx], lambda h: h.activation(out=x2[i][:], in_=yy[i][:], func=AF.Square))
            sc.op("dve", [kx], [kx], lambda h: h.tensor_scalar(out=x2[i][:], in0=x2[i][:], scalar1=0.044715, scalar2=1.0, op0=ALU.mult, op1=ALU.add))
            sc.op("dve", [kx, ky], [kx], lambda h: h.tensor_tensor(out=x2[i][:], in0=x2[i][:], in1=yy[i][:], op=ALU.mult))
            sc.op("act", [kx], [kx], lambda h: h.activation(out=x2[i][:], in_=x2[i][:], func=AF.Sigmoid, scale=1.5957691216057308))
            sc.op("dve", [kx, ky], [kz], lambda h: h.tensor_tensor(out=za[i][:], in0=x2[i][:], in1=yy[i][:], op=ALU.mult))
            sc.op("dve", [kz], [kzb], lambda h: h.tensor_copy(out=zb[i][:], in_=za[i][:]))
            for n in range(2):
                pst, pk = C.ps[(2 * t + n) % 4], f"ps{(2 * t + n) % 4}"
                def mm(h, n=n, pst=pst):
                    for k in range(2):
                        ins = h.matmul(pst[:], wg[:, k, n * 128:(n + 1) * 128], zb[i][:, k, :], start=(k == 0), stop=(k == 1))
                    return ins
                sc.op("pe", [kzb, "gwg"], [pk], mm)
                sk = f"gsg{n}"
                sc.op("act", [pk, "vecs"], [sk], lambda h, n=n, pst=pst: h.activation(out=sg[n][:], in_=pst[:], func=AF.Sigmoid, bias=V[:, 38 + n:39 + n]))
                sc.op("dve", [sk, kz], [ko], lambda h, n=n: h.tensor_tensor(out=ob[i][:, n, :], in0=za[i][:, n, :], in1=sg[n][:], op=ALU.mult))
            sc.dma("sp", S["Y"][768:1024, cs].rearrange("(k p) n -> p k n", p=128), ob[i][:], [ko], ["Y"])
    sc.barrier()


def ffn_up(C, I, S, L):
    sc, nc = C.sc, C.nc
    TG = 2048
    with ExitStack() as ph:
        C.ph = ph
        cp = psb(C, "fcp", [128, 44, 4])
        sc.dma("sp", cp[:], I["convp"][L], [], ["fcp"])
        halo = psb(C, "fhalo", [128, 44, 2])
        sc.op("dve", [], ["fhalo"], lambda h: h.memset(halo[:], 0.0))
        U = [psb(C, f"fU{i}", [128, TG + 2]) for i in range(3)]
        cv = [psb(C, f"fcv{i}", [128, TG]) for i in range(3)]
        tm = [psb(C, f"ftm{i}", [128, TG]) for i in range(2)]
        gs = [psb(C, f"fgs{i}", [128, TG]) for i in range(2)]
        ab = [psb(C, f"fab{i}", [128, TG], BF16) for i in range(2)]
        pend = []
        def evac(pst, pk, j, tok0, cnt):
            u_, uk = U[j % 3], f"fU{j % 3}"
            t = (tok0 % TG) // TT
            if t == 0:
                sc.op("dve", ["fhalo"], [uk], lambda h: h.tensor_copy(out=u_[:, 0:2], in_=halo[:, j, :]))
            sc.op("act", [pk], [uk], lambda h: h.activation(out=u_[:, 2 + t * TT:2 + (t + 1) * TT], in_=pst[:], func=AF.Identity))
        def after(j, g):
            u_, uk = U[j % 3], f"fU{j % 3}"
            c_, ck = cv[j % 3], f"fcv{j % 3}"
            t_, tk = tm[j % 2], f"ftm{j % 2}"
            gi = (j // 2) % 2
            while pend:
                pend.pop(0)()
            sc.op("dve", [uk], ["fhalo"], lambda h: h.tensor_copy(out=halo[:, j, :], in_=u_[:, TG:TG + 2]))
            sc.op("pool", [uk, "fcp"], [ck], lambda h: h.tensor_scalar(out=c_[:], in0=u_[:, 0:TG], scalar1=cp[:, j, 0:1], scalar2=cp[:, j, 3:4], op0=ALU.mult, op1=ALU.add))
            sc.op("dve", [uk, "fcp", ck], [ck], lambda h: h.scalar_tensor_tensor(out=c_[:], in0=u_[:, 1:TG + 1], scalar=cp[:, j, 1:2], in1=c_[:], op0=ALU.mult, op1=ALU.add))
            sc.op("dve", [uk, "fcp", ck], [ck], lambda h: h.scalar_tensor_tensor(out=c_[:], in0=u_[:, 2:TG + 2], scalar=cp[:, j, 2:3], in1=c_[:], op0=ALU.mult, op1=ALU.add))
            if j % 2 == 0:
                pend.append(lambda: sc.op("act", [ck], [f"fgs{gi}"], lambda h: h.activation(out=gs[gi][:], in_=c_[:], func=AF.Silu)))
            else:
                a_, ak = ab[gi], f"fab{gi}"
                sc.op("dve", [ck, f"fgs{gi}"], [ak], lambda h: h.tensor_tensor(out=a_[:], in0=gs[gi][:], in1=c_[:], op=ALU.mult))
                r0 = (j // 2) * 128
                sc.dma("sp", S["ACT"][r0:r0 + 128, g * TG:(g + 1) * TG], a_[:], [ak], ["ACT"])
        linear_fm(C, S["HN"], 0, 8, I["w_up"][L], 44, TG, evac, after=after, cast_eng="act")
    sc.barrier()


def ple(C, I, S, L):
    sc, nc = C.sc, C.nc
    with ExitStack() as ph:
        C.ph = ph
        pb16 = psb(C, "plp", [128, 2, 2048], BF16)
        ev = store_evac(C, lambda j: (S["PB"], 0, BF16)) if False else None
    with ExitStack() as ph:
        C.ph = ph
        tb = [psb(C, f"plt{i}", [128, 2, TT], BF16) for i in range(2)]
        for t in range(NTT):
            i = t % 2
            cs = slice(t * TT, (t + 1) * TT)
            sc.dma("pool", tb[i][:], I["pT"][L][:, cs].rearrange("(k p) n -> p k n", p=128), [], [f"plt{i}"])
            sc.dma("sp", S["PB"][:, cs].rearrange("(k p) n -> p k n", p=128), tb[i][:], [f"plt{i}"], ["PB"])
    sc.barrier()
    with ExitStack() as ph:
        C.ph = ph
        TG = 2048
        a = psb(C, "pla", [128, 8, TG], BF16)
        pa = psb(C, "plpa", [128, 2, TG], BF16)
        wg = [psb(C, f"plwg{i}", [128, 8, 128], BF16) for i in range(2)]
        wp = [psb(C, f"plwp{i}", [128, 2, 128], BF16) for i in range(2)]
        sg = [psb(C, f"plsg{i}", [128, TT]) for i in range(2)]
        hb = [psb(C, f"plh{i}", [128, TT]) for i in range(3)]
        Wg = I["w_pg"][L].rearrange("(k p) n -> p k n", p=128)
        Wp = I["w_ple"][L].rearrange("(k p) n -> p k n", p=128)
        cnt = 0
        for g in range(NT // TG):
            gs_ = slice(g * TG, (g + 1) * TG)
            sc.dma("sp", a[:], S["HN"][:, gs_].rearrange("(k p) n -> p k n", p=128), ["HN"], ["pla"])
            sc.dma("sp", pa[:], S["PB"][:, gs_].rearrange("(k p) n -> p k n", p=128), ["PB"], ["plpa"])
            for j in range(8):
                wi = j % 2
                sc.dma("pool", wg[wi][:], Wg[:, :, j * 128:(j + 1) * 128], [], [f"plwg{wi}"])
                sc.dma("pool", wp[wi][:], Wp[:, :, j * 128:(j + 1) * 128], [], [f"plwp{wi}"])
                for t in range(TG // TT):
                    p1, p2 = C.ps[(2 * cnt) % 4], C.ps[(2 * cnt + 1) % 4]
                    k1, k2 = f"ps{(2 * cnt) % 4}", f"ps{(2 * cnt + 1) % 4}"
                    hi_ = cnt % 3
                    si = cnt % 2
                    cnt += 1
                    ts_ = slice(t * TT, (t + 1) * TT)
                    def mm(h, p1=p1, p2=p2, wi=wi, ts_=ts_):
                        for k in range(8):
                            h.matmul(p1[:], wg[wi][:, k, :], a[:, k, ts_], start=(k == 0), stop=(k == 7))
                        for k in range(2):
                            ins = h.matmul(p2[:], wp[wi][:, k, :], pa[:, k, ts_], start=(k == 0), stop=(k == 1))
                        return ins
                    sc.op("pe", ["pla", "plpa", f"plwg{wi}", f"plwp{wi}"], [k1, k2], mm)
                    sc.op("act", [k1], [f"plsg{si}"], lambda h, p1=p1, si=si: h.activation(out=sg[si][:], in_=p1[:], func=AF.Sigmoid))
                    sc.op("dve", [k2, f"plsg{si}"], [f"plsg{si}"], lambda h, p2=p2, si=si: h.tensor_tensor(out=sg[si][:], in0=p2[:], in1=sg[si][:], op=ALU.mult))
                    hv = S["H"][j * 128:(j + 1) * 128, g * TG + t * TT:g * TG + (t + 1) * TT]
                    sc.dma("pool", hb[hi_][:], hv, ["H"], [f"plh{hi_}"])
                    sc.op("dve", [f"plsg{si}", f"plh{hi_}"], [f"plh{hi_}"], lambda h, si=si, hi_=hi_: h.tensor_tensor(out=hb[hi_][:], in0=hb[hi_][:], in1=sg[si][:], op=ALU.add))
                    sc.dma("sp", hv, hb[hi_][:], [f"plh{hi_}"], ["H"])
    sc.barrier()


_NC = None


def _prep(inp):
    f = lambda a: np.ascontiguousarray(np.asarray(a), dtype=np.float32)
    g = {k: np.asarray(v) for k, v in inp.items()}
    Lr = range(DEPTH)
    com = {}
    com["ident"] = np.eye(128, dtype=np.float32)
    am = np.zeros((128, 4, 512), np.float32)
    s_ = np.arange(128)[:, None]
    t_ = np.arange(512)[None, :]
    for m in range(4):
        am[:, m, :] = (128 * m + s_ <= t_)
    com["amask"] = am
    com["amadd"] = ((am - 1.0) * 30000.0).astype(np.float32)
    cv = np.zeros((128, 16), np.float32)
    cv[:, 0] = EPS
    cv[0:16, 1] = -1.0; cv[16:32, 1] = 1.0
    cv[0:64, 2] = 1.0
    cv[64:128, 3] = -1.0
    cv[64:128, 4] = 1.0
    cv[0:64, 5] = 1.0; cv[64:128, 5] = -1.0
    for k in range(8):
        cv[16 * k:16 * k + 16, 8 + k] = 1.0
    com["cvec"] = cv
    half = 16
    invf = (np.float32(10000.0) ** (-np.arange(half, dtype=np.float32) / np.float32(half))).astype(np.float32)
    com["invf"] = np.concatenate([invf, invf])[:, None].astype(np.float32)
    w_in = f(g["w_in"])
    sw = np.concatenate([np.arange(16, 32), np.arange(0, 16)])
    wfm = np.zeros((DEPTH, D, NIN_FM * 128), np.float32)
    wtm = np.zeros((DEPTH, D, 512), np.float32)
    wfm[:, :, 0:256] = w_in[:, :, 0:256]
    wfm[:, :, 256:384] = w_in[:, :, 256:384]
    wfm[:, :, 384 + 64:384 + 96] = w_in[:, :, 384:416]
    wfm[:, :, 512 + 64:512 + 96] = w_in[:, :, 384:416][:, :, sw]
    wfm[:, :, 640:896] = w_in[:, :, 416:672]
    wfm[:, :, 896:1152] = w_in[:, :, 672:928]
    wfm[:, :, 1152:1408] = w_in[:, :, 1188:1444]
    wfm[:, :, 1408:1664] = w_in[:, :, 1956:2212]
    wfm[:, :, 1664:1920] = w_in[:, :, 2212:2468]
    wfm[:, :, 1920:2176] = w_in[:, :, 1444:1700]
    wfm[:, :, 2176:2180] = w_in[:, :, 1184:1188]
    wtm[:, :, 0:256] = w_in[:, :, 928:1184]
    wtm[:, :, 256:512] = w_in[:, :, 1700:1956]
    com["w_in_fm"], com["w_in_tm"] = wfm, wtm
    wuq = f(g["mla_w_uq"]).reshape(DEPTH, 256, 4, 96)
    wq = np.zeros((DEPTH, 256, 4, 2, 128), np.float32)
    wq[:, :, :, 0, 0:96] = wuq
    wq[:, :, :, 1, 64:96] = wuq[:, :, :, 64:96][:, :, :, sw]
    com["w_uq"] = wq.reshape(DEPTH, 256, 1024)
    wukv = f(g["mla_w_ukv"]).reshape(DEPTH, 128, 4, 128)
    wk = np.zeros((DEPTH, 128, 4, 128), np.float32)
    wk[:, :, :, 0:64] = wukv[:, :, :, 0:64]
    com["w_uk"] = wk.reshape(DEPTH, 128, 512)
    com["w_uv"] = np.ascontiguousarray(wukv[:, :, :, 64:128]).reshape(DEPTH, 128, 256)
    com["w_out"] = f(g["w_out"])
    wup = f(g["w_up"]).reshape(DEPTH, D, 2, 22, 128).transpose(0, 1, 3, 2, 4)
    com["w_up"] = np.ascontiguousarray(wup).reshape(DEPTH, D, 2 * DFF)
    com["w_down"] = f(g["w_down"])
    com["w_pg"] = f(g["w_ple_gate"])
    com["w_ple"] = f(g["w_ple"])
    com["w_glu"] = f(g["s5_w_glu"])
    vec = np.zeros((DEPTH, 128, 64), np.float32)
    pk = lambda a, n: f(a).reshape(DEPTH, n, 128).transpose(0, 2, 1)
    vec[:, :, 0:8] = pk(g["attn_norm_g"], 8)
    vec[:, :, 8:16] = pk(g["ffn_norm_g"], 8)
    vec[:, :, 16:24] = pk(g["ple_norm_g"], 8)
    vec[:, :, 24:32] = pk(g["group_norm_g"], 8)
    vec[:, :, 32:34] = pk(g["mla_q_norm_g"], 2)
    vec[:, :, 34:35] = pk(g["mla_kv_norm_g"], 1)
    vec[:, 0:4, 35] = f(g["fox_b_f"])
    vec[:, :, 36:38] = pk(g["s5_d"], 2)
    vec[:, :, 38:40] = pk(g["s5_b_glu"], 2)
    com["vecs"] = vec
    cw = f(g["conv_w"]).reshape(DEPTH, 3, 2, 22, 128)
    cbi = f(g["conv_b"]).reshape(DEPTH, 1, 2, 22, 128)
    cpk = np.concatenate([cw, cbi], axis=1)
    com["convp"] = np.ascontiguousarray(cpk.transpose(0, 4, 3, 2, 1)).reshape(DEPTH, 128, 44, 4)
    lre, lim, lst = f(g["s5_lam_re"]), f(g["s5_lam_im"]), f(g["s5_log_step"])
    s5a = np.zeros((DEPTH, 128, 3, 16), np.float32)
    for hf in range(2):
        s5a[:, 64 * hf:64 * hf + 64, 0, :] = lre.transpose(0, 2, 1)
        s5a[:, 64 * hf:64 * hf + 64, 1, :] = lim.transpose(0, 2, 1)
        s5a[:, 64 * hf:64 * hf + 64, 2, :] = lst[:, None, :]
    com["s5a"] = s5a
    s5b = np.zeros((DEPTH, 128, 2, 5, 64), np.float32)
    bre, bim = f(g["s5_b_re"]), f(g["s5_b_im"])
    for gg in range(16):
        rows = slice(16 * (gg % 8), 16 * (gg % 8) + 16)
        tl = gg // 8
        s5b[:, rows, tl, 0, :] = lre[:, gg, None, :]
        s5b[:, rows, tl, 1, :] = lim[:, gg, None, :]
        s5b[:, rows, tl, 2, :] = lst[:, gg, None, None]
        s5b[:, rows, tl, 3, :] = bre[:, gg].transpose(0, 2, 1)
        s5b[:, rows, tl, 4, :] = bim[:, gg].transpose(0, 2, 1)
    com["s5b"] = s5b
    cre, cim = f(g["s5_c_re"]), f(g["s5_c_im"])
    com["s5c"] = np.ascontiguousarray(np.concatenate([cre.transpose(0, 3, 1, 2), cim.transpose(0, 3, 1, 2)], axis=1))
    com["lbp"] = np.ascontiguousarray(f(g["hgrn_lb_param"]).reshape(DEPTH, 2, 128).transpose(2, 1, 0))
    com["fing"] = np.ascontiguousarray(f(g["final_norm_g"]).reshape(8, 128).T)
    x, p, pos = f(g["x"]), f(g["p"]), np.asarray(g["positions"]).astype(np.int32)
    maps = []
    for c in range(8):
        b = c % 2
        m = dict(com)
        m["xT"] = np.ascontiguousarray(x[b].T)
        m["pT"] = np.ascontiguousarray(p[:, b].transpose(0, 2, 1))
        m["pos"] = np.ascontiguousarray(pos[b][None, :])
        maps.append(m)
    return maps


def kernel(**inputs):
    global _NC
    maps = _prep(inputs)
    if _NC is None:
        _NC = build()
    res = run_bass_kernel_spmd(_NC, maps, core_ids=list(range(8)))
    global _LAST
    _LAST = res.results
    outs = [np.asarray(res.results[b]["out"]).T for b in range(2)]
    return np.ascontiguousarray(np.stack(outs, 0)).astype(np.float32)
```
